# Optimizing a Trainium2 kernel written in Bass

```python
import jax, jax.numpy as jnp
from jax import lax
import numpy as np

D_MODEL = 1024
BATCH = 4
SEQ = 4096
DEPTH = 1

HEAD_DIM = 64
N_HEADS = D_MODEL // HEAD_DIM
N_MOBA_HEADS = N_HEADS // 2
N_FOX_HEADS = N_HEADS - N_MOBA_HEADS
MOBA_WIDTH = N_MOBA_HEADS * HEAD_DIM
FOX_WIDTH = N_FOX_HEADS * HEAD_DIM
IN_WIDTH = 3 * MOBA_WIDTH + 3 * FOX_WIDTH + N_FOX_HEADS
MOBA_BLOCK = 256
MOBA_TOPK = 3
MOBA_Q_CHUNK = 32
FOX_Q_BLOCK = 128
ROPE_THETA = 500000.0
ROPE_DIM = HEAD_DIM // 4
D_FF = 4 * D_MODEL
EPS = 1e-6

kernel_name = "hymba_moba_fox_sqrelu_adaln"


def _rms(x):
    xf = x.astype(jnp.float32)
    return xf * lax.rsqrt(jnp.mean(xf * xf, axis=-1, keepdims=True) + EPS)


def _head_norm(x, gain):
    return (_rms(x) * gain.astype(jnp.float32)).astype(x.dtype)


def _partial_rope(x):
    S = x.shape[2]
    half = ROPE_DIM // 2
    inv_freq = ROPE_THETA ** (-jnp.arange(0, ROPE_DIM, 2, dtype=jnp.float32) / ROPE_DIM)
    ang = jnp.arange(S, dtype=jnp.float32)[:, None] * inv_freq[None, :]
    cos, sin = jnp.cos(ang), jnp.sin(ang)
    xr = x[..., :ROPE_DIM].astype(jnp.float32)
    x1, x2 = xr[..., :half], xr[..., half:]
    rot = jnp.concatenate([x1 * cos - x2 * sin, x2 * cos + x1 * sin], axis=-1)
    return jnp.concatenate([rot.astype(x.dtype), x[..., ROPE_DIM:]], axis=-1)


def _split_heads(t, n_heads):
    B, S, _ = t.shape
    return t.reshape(B, S, n_heads, HEAD_DIM).transpose(0, 2, 1, 3)


def _merge_heads(t):
    B, H, S, d = t.shape
    return t.transpose(0, 2, 1, 3).reshape(B, S, H * d)


def moba_attention(q, k, v):
    B, H, S, d = q.shape
    nb = -(-S // MOBA_BLOCK)
    s_pad = nb * MOBA_BLOCK
    pad = ((0, 0), (0, 0), (0, s_pad - S), (0, 0))
    q, k, v = jnp.pad(q, pad), jnp.pad(k, pad), jnp.pad(v, pad)
    kb = k.reshape(B, H, nb, MOBA_BLOCK, d)
    vb = v.reshape(B, H, nb, MOBA_BLOCK, d)
    k_mean = jnp.mean(kb.astype(jnp.float32), axis=3)
    t_blk = jnp.arange(s_pad) // MOBA_BLOCK
    gate = jnp.einsum('bhtd,bhnd->bhtn', q.astype(jnp.float32), k_mean)
    fully_past = jnp.arange(nb)[None, :] < t_blk[:, None]
    gate = jnp.where(fully_past, gate, -jnp.inf)
    top_k = min(MOBA_TOPK, nb)
    _, sel = lax.top_k(gate, top_k)

    C = MOBA_Q_CHUNK
    n_chunks = s_pad // C
    q_c = q.reshape(B, H, n_chunks, C, d).transpose(2, 0, 1, 3, 4)
    sel_c = sel.reshape(B, H, n_chunks, C, top_k).transpose(2, 0, 1, 3, 4)
    b_idx = jnp.arange(B)[:, None, None, None]
    h_idx = jnp.arange(H)[None, :, None, None]
    local = jnp.arange(MOBA_BLOCK)
    scale = d ** -0.5
    n_sel = top_k * MOBA_BLOCK

    def chunk(args):
        qc, selc, ci = args
        t0 = ci * C
        blk = t0 // MOBA_BLOCK
        k_sel = kb[b_idx, h_idx, selc]
        v_sel = vb[b_idx, h_idx, selc]
        s_sel = jnp.einsum('bhtd,bhtkld->bhtkl', qc, k_sel,
                           preferred_element_type=jnp.float32) * scale
        valid = jnp.arange(top_k) < blk
        s_sel = jnp.where(valid[:, None], s_sel, -jnp.inf).reshape(B, H, C, n_sel)
        k_own = lax.dynamic_index_in_dim(kb, blk, axis=2, keepdims=False)
        v_own = lax.dynamic_index_in_dim(vb, blk, axis=2, keepdims=False)
        s_own = jnp.einsum('bhtd,bhld->bhtl', qc, k_own,
                           preferred_element_type=jnp.float32) * scale
        t_loc = t0 - blk * MOBA_BLOCK + jnp.arange(C)
        s_own = jnp.where(local[None, :] <= t_loc[:, None], s_own, -jnp.inf)
        p = jax.nn.softmax(jnp.concatenate([s_sel, s_own], axis=-1), axis=-1)
        p_sel = p[..., :n_sel].reshape(B, H, C, top_k, MOBA_BLOCK).astype(v.dtype)
        p_own = p[..., n_sel:].astype(v.dtype)
        return (jnp.einsum('bhtkl,bhtkld->bhtd', p_sel, v_sel)
                + jnp.einsum('bhtl,bhld->bhtd', p_own, v_own))

    out = lax.map(chunk, (q_c, sel_c, jnp.arange(n_chunks)))
    return out.transpose(1, 2, 0, 3, 4).reshape(B, H, s_pad, d)[:, :, :S]


def forgetting_attention(q, k, v, log_f):
    B, H, S, d = q.shape
    cum = jnp.cumsum(log_f, axis=-1)
    nq = S // FOX_Q_BLOCK
    q_b = q.reshape(B, H, nq, FOX_Q_BLOCK, d).transpose(2, 0, 1, 3, 4)
    c_b = cum.reshape(B, H, nq, FOX_Q_BLOCK).transpose(2, 0, 1, 3)
    key_pos = jnp.arange(S)
    scale = d ** -0.5

    def block(args):
        qb, cb, i = args
        s = jnp.einsum('bhtd,bhsd->bhts', qb, k, preferred_element_type=jnp.float32) * scale
        s = s + cb[..., :, None] - cum[:, :, None, :]
        q_pos = i * FOX_Q_BLOCK + jnp.arange(FOX_Q_BLOCK)
        s = jnp.where(key_pos[None, :] <= q_pos[:, None], s, -jnp.inf)
        p = jax.nn.softmax(s, axis=-1).astype(v.dtype)
        return jnp.einsum('bhts,bhsd->bhtd', p, v)

    out = lax.map(block, (q_b, c_b, jnp.arange(nq)))
    return out.transpose(1, 2, 0, 3, 4).reshape(B, H, S, d)


def setup_inputs(seed: int = 0) -> dict:
    key = jax.random.key(seed)
    ks = jax.random.split(key, 16)
    f32 = jnp.float32
    x = jax.random.normal(ks[0], (BATCH, SEQ, D_MODEL), f32)
    c = jax.random.normal(ks[1], (BATCH, D_MODEL), f32)
    w_ada = jax.random.normal(ks[2], (DEPTH, D_MODEL, 6 * D_MODEL), f32) * (0.5 * D_MODEL ** -0.5)
    b_ada = 0.01 * jax.random.normal(ks[3], (DEPTH, 6 * D_MODEL), f32)
    w_in = jax.random.normal(ks[4], (DEPTH, D_MODEL, IN_WIDTH), f32) * D_MODEL ** -0.5
    b_forget = jax.random.uniform(ks[5], (DEPTH, N_FOX_HEADS), f32, 1.0, 4.0)
    g_qn_moba = 1.0 + 0.02 * jax.random.normal(ks[6], (DEPTH, HEAD_DIM), f32)
    g_kn_moba = 1.0 + 0.02 * jax.random.normal(ks[7], (DEPTH, HEAD_DIM), f32)
    g_qn_fox = 1.0 + 0.02 * jax.random.normal(ks[8], (DEPTH, HEAD_DIM), f32)
    g_kn_fox = 1.0 + 0.02 * jax.random.normal(ks[9], (DEPTH, HEAD_DIM), f32)
    g_out_moba = 1.0 + 0.02 * jax.random.normal(ks[10], (DEPTH, MOBA_WIDTH), f32)
    g_out_fox = 1.0 + 0.02 * jax.random.normal(ks[11], (DEPTH, FOX_WIDTH), f32)
    w_out = jax.random.normal(ks[12], (DEPTH, D_MODEL, D_MODEL), f32) * D_MODEL ** -0.5
    w_ff1 = jax.random.normal(ks[13], (DEPTH, D_MODEL, D_FF), f32) * D_MODEL ** -0.5
    w_ff2 = jax.random.normal(ks[14], (DEPTH, D_FF, D_MODEL), f32) * D_FF ** -0.5
    return {"x": x, "c": c, "w_ada": w_ada, "b_ada": b_ada, "w_in": w_in, "b_forget": b_forget,
            "g_qn_moba": g_qn_moba, "g_kn_moba": g_kn_moba, "g_qn_fox": g_qn_fox, "g_kn_fox": g_kn_fox,
            "g_out_moba": g_out_moba, "g_out_fox": g_out_fox, "w_out": w_out,
            "w_ff1": w_ff1, "w_ff2": w_ff2}


def reference(x, c, w_ada, b_ada, w_in, b_forget, g_qn_moba, g_kn_moba, g_qn_fox, g_kn_fox,
              g_out_moba, g_out_fox, w_out, w_ff1, w_ff2):
    M = MOBA_WIDTH
    F = FOX_WIDTH
    for l in range(DEPTH):
        mod = jax.nn.silu(c) @ w_ada[l] + b_ada[l]
        sh_a, sc_a, g_a, sh_m, sc_m, g_m = [m[:, None, :] for m in jnp.split(mod, 6, axis=-1)]

        h = (_rms(x) * (1.0 + sc_a) + sh_a).astype(x.dtype)
        proj = h @ w_in[l]
        q_m = proj[..., 0:M]
        k_m = proj[..., M:2 * M]
        v_m = proj[..., 2 * M:3 * M]
        o0 = 3 * M
        q_f = proj[..., o0:o0 + F]
        k_f = proj[..., o0 + F:o0 + 2 * F]
        v_f = proj[..., o0 + 2 * F:o0 + 3 * F]
        f_logit = proj[..., o0 + 3 * F:]

        q_m = _partial_rope(_head_norm(_split_heads(q_m, N_MOBA_HEADS), g_qn_moba[l]))
        k_m = _partial_rope(_head_norm(_split_heads(k_m, N_MOBA_HEADS), g_kn_moba[l]))
        v_m = _split_heads(v_m, N_MOBA_HEADS)
        o_m = _merge_heads(moba_attention(q_m, k_m, v_m))

        q_f = _head_norm(_split_heads(q_f, N_FOX_HEADS), g_qn_fox[l])
        k_f = _head_norm(_split_heads(k_f, N_FOX_HEADS), g_kn_fox[l])
        v_f = _split_heads(v_f, N_FOX_HEADS)
        log_f = jax.nn.log_sigmoid(f_logit.astype(jnp.float32) + b_forget[l].astype(jnp.float32))
        o_f = _merge_heads(forgetting_attention(q_f, k_f, v_f, log_f.transpose(0, 2, 1)))

        mixed = jnp.concatenate([(_rms(o_m) * g_out_moba[l]).astype(x.dtype),
                                 (_rms(o_f) * g_out_fox[l]).astype(x.dtype)], axis=-1)
        x = x + g_a * (mixed @ w_out[l])

        h = (_rms(x) * (1.0 + sc_m) + sh_m).astype(x.dtype)
        x = x + g_m * (jnp.square(jax.nn.relu(h @ w_ff1[l])) @ w_ff2[l])
    return x
```

```python
import contextlib
import numpy as np
import concourse.bass as bass
import concourse.mybir as mybir
from concourse.bass_utils import run_bass_kernel_spmd

F32 = mybir.dt.float32
BF16 = mybir.dt.bfloat16
AF = mybir.ActivationFunctionType
ALU = mybir.AluOpType
AX = mybir.AxisListType
EPS = 1e-6
NEG = -30000.0
ROPE_THETA = 500000.0


class Tok:
    __slots__ = ("eng", "sem", "count")

    def __init__(self, eng, sem, count):
        self.eng = eng; self.sem = sem; self.count = count


class Buf:
    __slots__ = ("name", "w", "r")

    def __init__(self, name=""):
        self.name = name; self.w = None; self.r = {}


class Eng:
    def __init__(self, fw, eng, name, always_signal=True, ndma=8):
        self.fw = fw; self.eng = eng; self.name = name
        self.sem = fw.es.enter_context(fw.nc.semaphore("sem_" + name))
        self.n = 0
        self.waited = {}
        self.always = always_signal
        self.ndma = ndma
        self.dsems = []; self.dcnt = []; self.di = 0
        self.unsignaled = False

    def _wait_tok(self, tok):
        if tok is None:
            return
        if tok.eng is self and self.name == "pe":
            return
        key = id(tok.sem)
        if self.waited.get(key, 0) >= tok.count:
            return
        self.eng.wait_ge(tok.sem, tok.count)
        self.waited[key] = tok.count

    def _deps(self, reads, writes):
        for b in reads:
            self._wait_tok(b.w)
        for b in writes:
            self._wait_tok(b.w)
            for t in list(b.r.values()):
                self._wait_tok(t)

    def _record(self, tok, reads, writes):
        for b in reads:
            b.r[id(tok.sem)] = tok
        for b in writes:
            b.w = tok; b.r = {}

    def op(self, fn, reads=(), writes=(), signal=None):
        if self.fw.stopped:
            return None
        self._deps(reads, writes)
        ins = fn()
        sig = self.always if signal is None else signal
        if sig:
            self.n += 1
            ins.then_inc(self.sem, 1)
            tok = Tok(self, self.sem, self.n)
            self.unsignaled = False
        else:
            tok = Tok(self, self.sem, self.n + 1)
            self.unsignaled = True
        self._record(tok, reads, writes)
        return tok

    def dma(self, out, in_, reads=(), writes=(), **kw):
        if self.fw.stopped:
            return None
        self._deps(reads, writes)
        if len(self.dsems) < self.ndma:
            self.dsems.append(self.fw.es.enter_context(
                self.fw.nc.semaphore("dsem_%s_%d" % (self.name, len(self.dsems)))))
            self.dcnt.append(0)
        k = self.di % self.ndma
        self.di += 1
        sem = self.dsems[k]
        if self.dcnt[k] > 0:
            self._wait_tok(Tok(None, sem, self.dcnt[k]))
        self.dcnt[k] += 16
        self.eng.dma_start(out=out, in_=in_, **kw).then_inc(sem, 16)
        tok = Tok(None, sem, self.dcnt[k])
        self._record(tok, reads, writes)
        return tok

    def drain_dmas(self):
        for k, sem in enumerate(self.dsems):
            if self.dcnt[k] > 0:
                self._wait_tok(Tok(None, sem, self.dcnt[k]))


class FW:
    def __init__(self, nc, es):
        self.nc = nc; self.es = es
        self.stopped = False
        self.pe = Eng(self, nc.tensor, "pe", always_signal=False)
        self.act = Eng(self, nc.scalar, "act")
        self.dve = Eng(self, nc.vector, "dve")
        self.pool = Eng(self, nc.gpsimd, "pool")
        self.sp = Eng(self, nc.sync, "sp")
        self.engs = [self.pe, self.act, self.dve, self.pool, self.sp]

    def barrier(self):
        if self.stopped:
            return
        assert not self.pe.unsignaled, "PE has unsignaled trailing instructions"
        for e in self.engs:
            for f in self.engs:
                if f is not e and f.n > 0:
                    e._wait_tok(Tok(f, f.sem, f.n))
                for k, sem in enumerate(f.dsems):
                    if f.dcnt[k] > 0:
                        e._wait_tok(Tok(None, sem, f.dcnt[k]))


class _Stop(Exception):
    pass


class Cfg:
    stop = 0

    def __init__(self, D, S, DFF):
        self.D = D; self.S = S; self.DFF = DFF
        self.NM = D // 128; self.NF = D // 128


def build(cfg):
    D = cfg.D; S = cfg.S; DFF = cfg.DFF; NM = cfg.NM; NF = cfg.NF
    KC = D // 128; NT = S // 128; NB = S // 256; NQB = NB // 2; NOT = 2 * NQB
    Mw = 64 * NM; Fw = 64 * NF; INW = 3 * Mw + 3 * Fw + NF
    FC = DFF // 128
    NAUG = max(NB, 2); AUG = 64 + NAUG
    CW = min(512, D); NC2 = D // CW
    HT = NOT // 2
    assert Mw == Fw and NM % 2 == 0 and NF % 2 == 0 and NB >= 8 and (HT * 128) % 512 == 0

    nc = bass.Bass("TRN2", target_bir_lowering=False)

    def din(name, shape):
        return nc.dram_tensor(name, shape, F32, kind="ExternalInput").ap()

    xs = din("xs", [S, D]); ccol = din("ccol", [128, KC])
    w_ada = din("w_ada", [D, 6 * D]); b_ada = din("b_ada", [1, 6 * D])
    w_in = din("w_in", [D, INW]); w_out = din("w_out", [D, D])
    w_ff1 = din("w_ff1", [D, DFF]); w_ff2 = din("w_ff2", [DFF, D])
    bfg = din("b_forget", [1, NF])
    gqm = din("g_qn_moba", [1, 64]); gkm = din("g_kn_moba", [1, 64])
    gqf = din("g_qn_fox", [1, 64]); gkf = din("g_kn_fox", [1, 64])
    g_out = din("g_out", [1, D])
    ident = din("ident", [128, 128]); utri = din("utri", [128, 128])
    dmask = din("dmask", [128, 2 * 256]); cs = din("cs", [128, NT * 32])
    onehot = din("onehot", [NAUG, S]); foxaug = din("foxaug", [2, S])
    gmask = din("gmask", [1, NQB * NB]); pastvalid = din("pastvalid", [1, NQB * NB])
    negconst = din("negconst", [1, NQB * NB])
    out = nc.dram_tensor("out", [NOT * 128, D], F32, kind="ExternalOutput").ap()
    modscr = nc.dram_tensor("modscr", [1, 4 * D], F32, kind="Internal").ap()
    w1scr = nc.dram_tensor("w1scr", [DFF // 128, 128, D], BF16, kind="Internal").ap()
    w2scr = nc.dram_tensor("w2scr", [DFF, D], BF16, kind="Internal").ap()

    w_ada_v = w_ada.rearrange("(kc p) c -> p kc c", p=128)
    w_in_v = w_in.rearrange("(kc p) c -> p kc c", p=128)
    w_out_v = w_out.rearrange("(kc p) c -> p kc c", p=128)
    w_ff1_v = w_ff1.rearrange("(kc p) c -> p kc c", p=128)
    w_ff2_v = w_ff2.rearrange("(fc p) c -> p fc c", p=128)

    def cut(k):
        if cfg.stop == k:
            fw.stopped = True

    with contextlib.ExitStack() as es:
      fw = FW(nc, es)
      try:
          PE, ACT, DVE, POOL, SP = fw.pe, fw.act, fw.dve, fw.pool, fw.sp
          T = nc.tensor; A = nc.scalar; V = nc.vector; G = nc.gpsimd

          uniq = [0]

          def sb(name, shape, dt=F32, stack=es):
              uniq[0] += 1
              return stack.enter_context(nc.sbuf_tensor("%s_%d" % (name, uniq[0]), shape, dt))

          def pm(name, shape, dt=F32, stack=es):
              uniq[0] += 1
              esz = 4 if dt == F32 else 2
              nfree = 1
              for d_ in shape[1:]:
                  nfree *= d_
              nel = ((nfree * esz + 2047) // 2048) * 2048 // esz
              t_ = stack.enter_context(nc.psum_tensor("%s_%d" % (name, uniq[0]), [128, nel], dt))
              ap = t_[0:shape[0], 0:nfree]
              if len(shape) == 3:
                  ap = ap.rearrange("p (a b) -> p a b", a=shape[1])
              return ap

          def rstd_ops(dst, src, n, bsrc, bdst, tmp, btmp):
              ACT.op(lambda: A.activation(out=tmp, in_=src, func=AF.Ln, scale=1.0 / n, bias=epsb[:, 0:1]),
                     [bsrc, bepsb], [btmp])
              ACT.op(lambda: A.activation(out=dst, in_=tmp, func=AF.Exp, scale=-0.5), [btmp], [bdst])

          identf = sb("identf", [128, 128]); identb = sb("identb", [128, 128], BF16)
          utrif = sb("utrif", [128, 128]); onesf = sb("onesf", [128, 128])
          epsb = sb("epsb", [128, 1])
          dmaskb = sb("dmaskb", [128, 2, 256], BF16)
          cst = sb("cst", [128, NT, 32])
          gqm_b = sb("gqm_b", [128, 4, 64]); gkm_b = sb("gkm_b", [128, 4, 64])
          gqf_b = sb("gqf_b", [128, 4, 64]); gkf_b = sb("gkf_b", [128, 4, 64])
          goutbc = sb("goutbc", [128, D]); gbc = sb("gbc", [128, 2, D])
          modcol = sb("modcol", [128, 4 * KC])
          cum = sb("cum", [128, NT, NF]); negcum = sb("negcum", [128, NT, NF])
          gmaskb = sb("gmaskb", [128, NQB * NB]); pvb = sb("pvb", [128, NQB * NB]); ncb = sb("ncb", [128, NQB * NB])
          bfb = sb("bfb", [128, NF])
          Obuf = sb("Obuf", [128, NOT, D], BF16)
          bconst = Buf("const"); bepsb = Buf("eps"); bmodcol = Buf("modcol"); bmodscr = Buf("modscr"); bgbc = Buf("gbc"); bcum = Buf("cum")
          bO = [Buf("O%d" % i) for i in range(NOT)]
          bout = [Buf("out%d" % i) for i in range(NOT)]

          SP.dma(identf[:], ident[:, :], writes=[bconst])
          SP.dma(utrif[:], utri[:, :], writes=[bconst])
          SP.dma(cst[:].rearrange("p t c -> p (t c)"), cs[:, :], writes=[bconst])
          for gt_, gsrc in ((gqm_b, gqm), (gkm_b, gkm), (gqf_b, gqf), (gkf_b, gkf)):
              for hd in range(4):
                  SP.dma(gt_[:, hd, :], gsrc.partition_broadcast(128), writes=[bconst])
          SP.dma(goutbc[:], g_out.partition_broadcast(128), writes=[bconst])
          SP.dma(gmaskb[:], gmask.partition_broadcast(128), writes=[bconst])
          SP.dma(pvb[:], pastvalid.partition_broadcast(128), writes=[bconst])
          SP.dma(ncb[:], negconst.partition_broadcast(128), writes=[bconst])
          SP.dma(bfb[:], bfg.partition_broadcast(128), writes=[bconst])
          POOL.dma(identb[:], ident[:, :], writes=[bconst])
          POOL.dma(dmaskb[:].rearrange("p a q -> p (a q)"), dmask[:, :], writes=[bconst])
          DVE.op(lambda: V.memset(onesf[:], 1.0), [], [bconst])
          DVE.op(lambda: V.memset(epsb[:], EPS), [], [bepsb])
          cut(1)

          with contextlib.ExitStack() as sa:
              csil = sb("csil", [128, KC], stack=sa); scl = sb("scl", [128, KC], stack=sa)
              lbc = sb("lbc", [128, KC, 128], stack=sa)
              wada = [sb("wada%d" % i, [128, KC, 512], stack=sa) for i in range(6)]
              modbc = sb("modbc", [128, 6 * D], stack=sa)
              pmod = [pm("pmod%d" % i, [128, 512], stack=sa) for i in range(4)]
              pcol = pm("pcol", [128, 4 * KC], stack=sa)
              bcs = Buf(); bscl = Buf(); blbc = Buf(); bmod = Buf(); bpcol = Buf()
              bwada = [Buf() for _ in range(6)]; bpmod = [Buf() for _ in range(4)]
              SP.dma(csil[:], ccol[:, :], writes=[bcs])
              SP.dma(modbc[:], b_ada.partition_broadcast(128), writes=[bmod])
              ACT.op(lambda: A.activation(out=scl[:], in_=csil[:], func=AF.Silu), [bcs], [bscl])
              for kc in range(KC):
                  DVE.op(lambda kc=kc: V.tensor_copy(out=lbc[:, kc, :], in_=scl[:, kc:kc + 1].to_broadcast([128, 128])),
                         [bscl], [blbc])
              NCH = 6 * D // 512
              for ch in range(NCH):
                  sl = ch % 6; pl = ch % 4
                  (SP if ch % 2 == 0 else ACT).dma(wada[sl][:], w_ada_v[:, :, ch * 512:(ch + 1) * 512], writes=[bwada[sl]])
                  for kc in range(KC):
                      PE.op(lambda kc=kc: T.matmul(pmod[pl][:], lhsT=lbc[:, kc, :], rhs=wada[sl][:, kc, :],
                                                   start=(kc == 0), stop=(kc == KC - 1)),
                            [blbc, bwada[sl]], [bpmod[pl]], signal=(kc == KC - 1))
                  DVE.op(lambda: V.tensor_tensor(out=modbc[:, ch * 512:(ch + 1) * 512], in0=pmod[pl][:],
                                                 in1=modbc[:, ch * 512:(ch + 1) * 512], op=ALU.add),
                         [bpmod[pl], bmod], [bmod])
              for vi, v in enumerate([0, 1, 3, 4]):
                  if v in (1, 4):
                      DVE.op(lambda v=v: V.tensor_scalar_add(out=modbc[0:1, v * D:(v + 1) * D], in0=modbc[0:1, v * D:(v + 1) * D],
                                                             scalar1=1.0), [bmod], [bmod])
                  SP.dma(modscr[0:1, vi * D:(vi + 1) * D], modbc[0:1, v * D:(v + 1) * D], reads=[bmod], writes=[bmodscr])
              DVE.op(lambda: V.tensor_copy(out=gbc[:, 0, :], in_=modbc[:, 2 * D:3 * D]), [bmod], [bgbc])
              DVE.op(lambda: V.tensor_copy(out=gbc[:, 1, :], in_=modbc[:, 5 * D:6 * D]), [bmod], [bgbc])
              fw.barrier()
              cut(2)

          with contextlib.ExitStack() as sbd:
              hT = sb("hT", [128, KC, S], BF16, stack=sbd)
              bhT = [Buf("hT%d" % t) for t in range(NT)]
              stat = sb("stat", [128, 16], stack=sbd)
              bstat = [Buf() for _ in range(16)]

              with contextlib.ExitStack() as sB:
                  NB_ = 3
                  junk = sb("junk", [128, D], stack=sB); bjunk = Buf()
                  mba = sb("mba", [128, 2, D], stack=sB); bmba = Buf()
                  xt = [sb("xt%d" % i, [128, D], stack=sB) for i in range(8)]
                  xn = [sb("xn%d" % i, [128, D], stack=sB) for i in range(NB_)]
                  hb = [sb("hb%d" % i, [128, D], BF16, stack=sB) for i in range(NB_)]
                  stB = sb("stB", [128, 8, 4], stack=sB)
                  bsB = [[Buf() for _ in range(3)] for _ in range(8)]
                  ptr = [pm("ptr%d" % i, [128, KC * 128], BF16, stack=sB) for i in range(2)]
                  bxt = [Buf() for _ in range(8)]; bxn = [Buf() for _ in range(NB_)]; bhb = [Buf() for _ in range(NB_)]
                  bptr = [Buf(), Buf()]
                  SP.dma(mba[:, 0, :], modscr[0:1, 0:D].partition_broadcast(128), reads=[bmodscr], writes=[bmba])
                  SP.dma(mba[:, 1, :], modscr[0:1, D:2 * D].partition_broadcast(128), reads=[bmodscr], writes=[bmba])

                  def b_s1(t):
                      sl = t % NB_; xl = t % 8
                      SP.dma(xt[xl][:], xs[t * 128:(t + 1) * 128, :], writes=[bxt[xl]])
                      ACT.op(lambda: A.activation(out=junk[:], in_=xt[xl][:], func=AF.Square,
                                                  accum_out=stB[:, xl, 0:1]), [bxt[xl]], [bjunk, bsB[xl][0]])

                  def b_s2(t):
                      sl = t % NB_; xl = t % 8
                      rstd_ops(stB[:, xl, 2:3], stB[:, xl, 0:1], D, bsB[xl][0], bsB[xl][2], stB[:, xl, 1:2], bsB[xl][1])
                      DVE.op(lambda: V.scalar_tensor_tensor(out=xn[sl][:], in0=xt[xl][:], scalar=stB[:, xl, 2:3],
                                                            in1=mba[:, 1, :], op0=ALU.mult, op1=ALU.mult),
                             [bxt[xl], bsB[xl][2], bmba], [bxn[sl]])
                      DVE.op(lambda: V.tensor_tensor(out=hb[sl][:], in0=xn[sl][:], in1=mba[:, 0, :], op=ALU.add),
                             [bxn[sl], bmba], [bhb[sl]])

                  def b_s3(t):
                      sl = t % NB_; ps_ = t % 2
                      for kc in range(KC):
                          PE.op(lambda kc=kc: T.transpose(out=ptr[ps_][:, kc * 128:(kc + 1) * 128],
                                                          in_=hb[sl][:, kc * 128:(kc + 1) * 128], identity=identb[:]),
                                [bhb[sl], bconst], [bptr[ps_]], signal=(kc == KC - 1))
                      ACT.op(lambda: A.copy(out=hT[:, :, t * 128:(t + 1) * 128],
                                            in_=ptr[ps_][:].rearrange("p (k c) -> p k c", k=KC)),
                             [bptr[ps_]], [bhT[t]])

                  for k in range(NT + 6):
                      if k < NT:
                          b_s1(k)
                      if 0 <= k - 5 < NT:
                          b_s2(k - 5)
                      if 0 <= k - 6 < NT:
                          b_s3(k - 6)
                  fw.barrier()
                  cut(3)

              with contextlib.ExitStack() as sC:
                  NN = NT * NF
                  wf = sb("wf", [128, KC, NF], BF16, stack=sC)
                  zf = sb("zf", [128, NN], stack=sC); ef = sb("ef", [128, NN], stack=sC)
                  lf = sb("lf", [128, NN], stack=sC); Tsb = sb("Tsb", [128, NN], stack=sC)
                  carry = sb("carry", [128, NT, NF], stack=sC)
                  pfl = pm("pfl", [128, NN], stack=sC); pW = pm("pW", [128, NN], stack=sC); pT = pm("pT", [128, NN], stack=sC)
                  bwf = Buf(); bzf = Buf(); bef = Buf(); blf = Buf(); bTsb = Buf(); bcarry = Buf()
                  bpfl = Buf(); bpW = Buf(); bpT = Buf()
                  POOL.dma(wf[:], w_in_v[:, :, INW - NF:INW], writes=[bwf])
                  for t in range(NT):
                      for kc in range(KC):
                          PE.op(lambda t=t, kc=kc: T.matmul(pfl[:, t * NF:(t + 1) * NF],
                                                            lhsT=hT[:, kc, t * 128:(t + 1) * 128], rhs=wf[:, kc, :],
                                                            start=(kc == 0), stop=(kc == KC - 1)),
                                [bhT[t], bwf], [bpfl], signal=(t == NT - 1 and kc == KC - 1))
                  DVE.op(lambda: V.tensor_tensor(out=zf[:].rearrange("p (t f) -> p t f", f=NF),
                                                 in0=pfl[:].rearrange("p (t f) -> p t f", f=NF),
                                                 in1=bfb[:].unsqueeze(1).to_broadcast([128, NT, NF]), op=ALU.add),
                         [bpfl, bconst], [bzf])
                  ACT.op(lambda: A.activation(out=ef[:], in_=zf[:], func=AF.Exp, scale=-1.0), [bzf], [bef])
                  ACT.op(lambda: A.activation(out=zf[:], in_=ef[:], func=AF.Ln, scale=1.0, bias=onesf[:, 0:1]),
                         [bef, bconst], [bzf])
                  DVE.op(lambda: V.tensor_scalar_mul(out=lf[:], in0=zf[:], scalar1=-1.0), [bzf], [blf])
                  PE.op(lambda: T.matmul(pW[:], lhsT=utrif[:], rhs=lf[:], start=True, stop=True), [blf, bconst], [bpW], signal=True)
                  PE.op(lambda: T.matmul(pT[:], lhsT=onesf[:], rhs=lf[:], start=True, stop=True), [blf, bconst], [bpT], signal=True)
                  DVE.op(lambda: V.tensor_copy(out=Tsb[:], in_=pT[:]), [bpT], [bTsb])
                  DVE.op(lambda: V.memset(carry[:, 0, :], 0.0), [], [bcarry])
                  for i in range(1, NT):
                      DVE.op(lambda i=i: V.tensor_tensor(out=carry[:, i, :], in0=carry[:, i - 1, :],
                                                         in1=Tsb[:, (i - 1) * NF:i * NF], op=ALU.add),
                             [bcarry, bTsb], [bcarry])
                  DVE.op(lambda: V.tensor_tensor(out=cum[:].rearrange("p t f -> p (t f)"), in0=pW[:],
                                                 in1=carry[:].rearrange("p t f -> p (t f)"), op=ALU.add),
                         [bpW, bcarry], [bcum])
                  DVE.op(lambda: V.tensor_scalar_mul(out=negcum[:].rearrange("p t f -> p (t f)"),
                                                     in0=cum[:].rearrange("p t f -> p (t f)"), scalar1=-1.0),
                         [bcum], [bcum])
                  fw.barrier()
                  cut(4)

              KTall = sb("KTall", [128, 2, S], BF16, stack=sbd)
              QTall = sb("QTall", [128, 2, NOT * 128], BF16, stack=sbd)
              VAp = sb("VAp", [128, NT + 1, 2, 80], BF16, stack=sbd)
              qtok = sb("qtok", [128, NOT, 2, AUG], BF16, stack=sbd)
              wq = [sb("wq%d" % i, [128, KC, 128], BF16, stack=sbd) for i in range(2)]
              wkv = [sb("wkv%d" % i, [128, KC, 256], BF16, stack=sbd) for i in range(2)]
              NS = 4; LAG = 2
              knf = [sb("knf%d" % i, [128, 4, 64], stack=sbd) for i in range(NS)]
              kb = [sb("kb%d" % i, [128, 4, 64], BF16, stack=sbd) for i in range(NS)]
              rp = [sb("rp%d" % i, [128, 2, 4, 16], stack=sbd) for i in range(NS)]
              statp = sb("statp", [128, NS, 12], stack=sbd)
              junkp = sb("junkp", [128, NS, 256], stack=sbd)
              bstp = [[Buf() for _ in range(3)] for _ in range(NS)]
              bjunkp = [Buf() for _ in range(NS)]
              bjunkq = [[Buf(), Buf()] for _ in range(NS)]
              bstq = [[Buf(), Buf()] for _ in range(NS)]
              CgK = sb("CgK", [128, NT, 16], stack=sbd); SgK = sb("SgK", [128, NT, 16], stack=sbd)
              CgQ = sb("CgQ", [128, NT, 16], stack=sbd); SgQ = sb("SgQ", [128, NT, 16], stack=sbd)
              brt = Buf()
              for Cg_, Sg_, gb_ in ((CgK, SgK, gkm_b), (CgQ, SgQ, gqm_b)):
                  DVE.op(lambda: V.tensor_tensor(out=Cg_[:], in0=cst[:, :, 0:16],
                                                 in1=gb_[:, 0, 0:16].unsqueeze(1).to_broadcast([128, NT, 16]), op=ALU.mult),
                         [bconst], [brt])
                  DVE.op(lambda: V.tensor_tensor(out=Sg_[:, :, 0:8], in0=cst[:, :, 16:24],
                                                 in1=gb_[:, 0, 8:16].unsqueeze(1).to_broadcast([128, NT, 8]), op=ALU.mult),
                         [bconst], [brt])
                  DVE.op(lambda: V.tensor_tensor(out=Sg_[:, :, 8:16], in0=cst[:, :, 24:32],
                                                 in1=gb_[:, 0, 0:8].unsqueeze(1).to_broadcast([128, NT, 8]), op=ALU.mult),
                         [bconst], [brt])
              kms = [sb("kms%d" % i, [64, NB], stack=sbd) for i in range(2)]
              kmT = [sb("kmT%d" % i, [64, NB], BF16, stack=sbd) for i in range(2)]
              gms = [sb("gms%d" % i, [128, 4, NB], stack=sbd) for i in range(2)]; mx8 = [sb("mx8%d" % i, [128, 4, 8], stack=sbd) for i in range(2)]
              nsf = [sb("nsf%d" % i, [128, 4, NB], stack=sbd) for i in range(2)]
              pTs = [sb("pTs%d" % i, [128, 512], BF16, stack=sbd) for i in range(3)]
              osb = [sb("osb%d" % i, [65, 512], stack=sbd) for i in range(2)]
              rden = sb("rden", [128, 4], stack=sbd)
              bKT = [Buf() for _ in range(NT // 4)]
              bKTaug = Buf()
              bQT = [Buf() for _ in range(NQB)]
              bVA = [Buf() for _ in range(NT)]
              bqtok = [Buf() for _ in range(NOT)]
              bwq = [Buf(), Buf()]; bwkv = [Buf(), Buf()]
              bknf = [Buf() for _ in range(NS)]; bkb = [Buf() for _ in range(NS)]; brp = [Buf() for _ in range(NS)]
              bkms = [Buf(), Buf()]; bkmT = [Buf(), Buf()]; bgms = [Buf(), Buf()]; bmx8 = [Buf(), Buf()]; bnsf = [Buf(), Buf()]
              bpTs = [Buf() for _ in range(3)]; bosb = [Buf(), Buf()]; brden = Buf()
              DVE.op(lambda: V.memset(VAp[:].rearrange("p t h c -> p (t h c)"), 1.0), [], bVA)
              VApf = VAp[:].rearrange("p t h c -> p (t h c)")
              DVE.op(lambda: V.memset(KTall[64:128, :, :].rearrange("p h s -> p (h s)"), 0.0), [], [bKTaug])
              DVE.op(lambda: V.memset(QTall[64:128, :, :].rearrange("p h s -> p (h s)"), 0.0), [], bQT)
              DVE.op(lambda: V.memset(qtok[:].rearrange("p t h c -> p (t h c)"), 1.0), [], bqtok)

              def v4(ap3):
                  return ap3.rearrange("p (a h) c -> p a h c", a=2)

              def normrope(src4, gain_b, Cg, Sg, is_moba, t0, sl, dst4, reads, writes):
                  ACT.op(lambda: A.activation(out=v4(junkp[:, sl, :].rearrange("p (a c) -> p a c", c=64)), in_=src4,
                                              func=AF.Square), reads, [bjunkp[sl]])
                  DVE.op(lambda: V.tensor_reduce(out=statp[:, sl, 0:4], in_=junkp[:, sl, :].rearrange("p (a c) -> p a c", c=64),
                                                 axis=AX.X, op=ALU.add), [bjunkp[sl]], [bstp[sl][0]])
                  ACT.op(lambda: A.activation(out=statp[:, sl, 4:8], in_=statp[:, sl, 0:4], func=AF.Sqrt,
                                              scale=1.0 / 64.0, bias=epsb[:, 0:1]),
                         [bstp[sl][0], bepsb], [bstp[sl][1]])
                  DVE.op(lambda: V.reciprocal(out=statp[:, sl, 8:12], in_=statp[:, sl, 4:8]), [bstp[sl][1]], [bstp[sl][2]])
                  DVE.op(lambda: V.tensor_tensor(out=v4(knf[sl][:]), in0=src4,
                                                 in1=v4(statp[:, sl, 8:12].unsqueeze(2).to_broadcast([128, 4, 64])),
                                                 op=ALU.mult), reads + [bstp[sl][2]], [bknf[sl]])
                  if not is_moba:
                      DVE.op(lambda: V.tensor_tensor(out=dst4, in0=v4(knf[sl][:]), in1=v4(gain_b[:]), op=ALU.mult),
                             [bknf[sl], bconst], writes)
                      return
                  DVE.op(lambda: V.tensor_tensor(out=dst4[:, :, :, 16:64], in0=v4(knf[sl][:, :, 16:64]),
                                                 in1=v4(gain_b[:, :, 16:64]), op=ALU.mult), [bknf[sl], bconst], writes)
                  r = rp[sl]
                  k4 = v4(knf[sl][:])
                  POOL.op(lambda: G.tensor_tensor(out=v4(r[:, 0, :, :]), in0=k4[:, :, :, 0:16],
                                                  in1=Cg[:, t0:t0 + 2, :].unsqueeze(2).to_broadcast([128, 2, 2, 16]), op=ALU.mult),
                          [bknf[sl], brt], [brp[sl]])
                  POOL.op(lambda: G.tensor_tensor(out=v4(r[:, 1, :, 0:8]), in0=k4[:, :, :, 8:16],
                                                  in1=Sg[:, t0:t0 + 2, 0:8].unsqueeze(2).to_broadcast([128, 2, 2, 8]), op=ALU.mult),
                          [bknf[sl], brt], [brp[sl]])
                  POOL.op(lambda: G.tensor_tensor(out=v4(r[:, 1, :, 8:16]), in0=k4[:, :, :, 0:8],
                                                  in1=Sg[:, t0:t0 + 2, 8:16].unsqueeze(2).to_broadcast([128, 2, 2, 8]), op=ALU.mult),
                          [bknf[sl], brt], [brp[sl]])
                  POOL.op(lambda: G.tensor_tensor(out=dst4[:, :, :, 0:16], in0=v4(r[:, 0, :, :]), in1=v4(r[:, 1, :, :]), op=ALU.add),
                          [brp[sl]], writes)

              n_pairs = NM // 2 + NF // 2
              bw1scr = Buf("w1scr"); bw2scr = Buf("w2scr")
              bg_jobs = []
              w1scr_pv = w1scr.rearrange("f p x -> p f x")
              FQ = 8 if FC % 8 == 0 else FC
              for kc_ in range(KC):
                  for f0_ in range(0, FC, FQ):
                      bg_jobs.append(lambda kc_=kc_, f0_=f0_: POOL.dma(
                          w1scr_pv[:, f0_:f0_ + FQ, kc_ * 128:(kc_ + 1) * 128],
                          w_ff1[kc_ * 128:(kc_ + 1) * 128, f0_ * 128:(f0_ + FQ) * 128].rearrange("p (f c) -> p f c", c=128),
                          writes=[bw1scr]))
              for r_ in range(DFF // 512):
                  bg_jobs.append(lambda r_=r_: POOL.dma(w2scr[r_ * 512:(r_ + 1) * 512, :], w_ff2[r_ * 512:(r_ + 1) * 512, :],
                                                        writes=[bw2scr]))
              for u in range(n_pairs):
                  is_moba = u < NM // 2
                  if is_moba:
                      hbase = 2 * u
                      qc0 = hbase * 64; kc0 = Mw + hbase * 64; vc0 = 2 * Mw + hbase * 64
                      gq_b, gk_b = gqm_b, gkm_b
                      KA = 64 + NB
                  else:
                      fh0 = 2 * (u - NM // 2)
                      hbase = NM + fh0
                      qc0 = 3 * Mw + fh0 * 64; kc0 = 3 * Mw + Fw + fh0 * 64; vc0 = 3 * Mw + 2 * Fw + fh0 * 64
                      gq_b, gk_b = gqf_b, gkf_b
                      KA = 66
                  ws = u % 2
                  POOL.dma(wq[ws][:], w_in_v[:, :, qc0:qc0 + 128], writes=[bwq[ws]])
                  POOL.dma(wkv[ws][:, :, 0:128], w_in_v[:, :, kc0:kc0 + 128], writes=[bwkv[ws]])
                  POOL.dma(wkv[ws][:, :, 128:256], w_in_v[:, :, vc0:vc0 + 128], writes=[bwkv[ws]])
                  if not is_moba:
                      nbg = (len(bg_jobs) + n_pairs - 1 - u) // (n_pairs - u) if bg_jobs else 0
                      for _ in range(nbg):
                          bg_jobs.pop(0)()
                  if u == 0:
                      for hd in range(2):
                          POOL.dma(KTall[64:64 + NAUG, hd, :], onehot[:, :], writes=[bKTaug])
                  if u == NM // 2:
                      DVE.op(lambda: V.memset(KTall[64:128, :, :].rearrange("p h s -> p (h s)"), 0.0), [], [bKTaug])
                      for hd in range(2):
                          POOL.dma(KTall[64:66, hd, :], foxaug[:, :], writes=[bKTaug])
                      DVE.op(lambda: V.memset(qtok[:].rearrange("p t h c -> p (t h c)"), 1.0), [], bqtok)

                  with contextlib.ExitStack() as sP:
                      pkv = [pm("pkv%d" % i, [128, 2, 256], stack=sP) for i in range(NS)]
                      pKT = [pm("pKT%d" % i, [64, 2, 512], BF16, stack=sP) for i in range(2)]
                      pQT = pm("pQT", [AUG, 2, 256], BF16, stack=sP)
                      bpkv = [Buf() for _ in range(NS)]; bpKT = [Buf(), Buf()]; bpQT = Buf()
                      Cgk, Sgk, Cgq, Sgq = CgK, SgK, CgQ, SgQ

                      def kv_stage1(tp):
                          sl = tp % NS; t0 = 2 * tp
                          for a in range(2):
                              t = t0 + a
                              for kc in range(KC):
                                  PE.op(lambda kc=kc: T.matmul(pkv[sl][:, a, :], lhsT=hT[:, kc, t * 128:(t + 1) * 128],
                                                               rhs=wkv[ws][:, kc, :], start=(kc == 0), stop=(kc == KC - 1)),
                                        [bhT[t], bwkv[ws]], [bpkv[sl]], signal=(a == 1 and kc == KC - 1))
                          vsrc = pkv[sl][:, :, 128:256].rearrange("p a (h c) -> p a h c", h=2)
                          ksrc = pkv[sl][:, :, 0:128].rearrange("p a (h c) -> p a h c", h=2)
                          if is_moba:
                              ACT.op(lambda: A.copy(out=VAp[:, t0:t0 + 2, :, 0:64], in_=vsrc),
                                     [bpkv[sl]], [bVA[t0], bVA[t0 + 1]])
                          normrope(ksrc, gk_b, Cgk, Sgk, is_moba, t0, sl, v4(kb[sl][:]), [bpkv[sl]], [bkb[sl]])
                          if not is_moba:
                              DVE.op(lambda: V.tensor_copy(out=VAp[:, t0:t0 + 2, :, 0:64], in_=vsrc),
                                     [bpkv[sl]], [bVA[t0], bVA[t0 + 1]])

                      def kv_stage2(tp):
                          sl = tp % NS; t0 = 2 * tp
                          g4 = t0 // 4; gs = g4 % 2
                          for a in range(2):
                              t = t0 + a
                              for hd in range(2):
                                  PE.op(lambda hd=hd: T.transpose(out=pKT[gs][:, hd, (t % 4) * 128:(t % 4 + 1) * 128],
                                                                  in_=kb[sl][:, a * 2 + hd, :], identity=identb[:]),
                                        [bkb[sl], bconst], [bpKT[gs]], signal=(t % 4 == 3 and hd == 1))
                          if (t0 + 1) % 4 == 3:
                              if g4 % 2 == 0:
                                  ACT.op(lambda: A.copy(out=KTall[0:64, :, g4 * 512:(g4 + 1) * 512], in_=pKT[gs][:, :, :]),
                                         [bpKT[gs]], [bKT[g4]])
                              else:
                                  DVE.op(lambda: V.tensor_copy(out=KTall[0:64, :, g4 * 512:(g4 + 1) * 512], in_=pKT[gs][:, :, :]),
                                         [bpKT[gs]], [bKT[g4]])

                      NTP = NT // 2
                      for tt in range(NTP + 2):
                          if tt < NTP:
                              kv_stage1(tt)
                          if tt >= 2:
                              kv_stage2(tt - 2)
                      cut(51)

                      if is_moba:
                          for hd in range(2):
                              DVE.op(lambda hd=hd: V.tensor_reduce(out=kms[hd][:], in_=KTall[0:64, hd, :].rearrange("p (n l) -> p n l", l=256),
                                                                   axis=AX.X, op=ALU.add), bKT, [bkms[hd]])
                              DVE.op(lambda hd=hd: V.tensor_scalar_mul(out=kmT[hd][:], in0=kms[hd][:], scalar1=1.0 / 256.0),
                                     [bkms[hd]], [bkmT[hd]])

                      def q_stage1(j):
                          sl = j % NS
                          gt0 = 4 * j + 2
                          for a in range(2):
                              gt = gt0 + a
                              for kc in range(KC):
                                  PE.op(lambda kc=kc: T.matmul(pkv[sl][:, a, 0:128], lhsT=hT[:, kc, gt * 128:(gt + 1) * 128],
                                                               rhs=wq[ws][:, kc, :], start=(kc == 0), stop=(kc == KC - 1)),
                                        [bhT[gt], bwq[ws]], [bpkv[sl]], signal=(a == 1 and kc == KC - 1))
                          qsrc = pkv[sl][:, :, 0:128].rearrange("p a (h c) -> p a h c", h=2)
                          normrope(qsrc, gq_b, Cgq, Sgq, is_moba, gt0, sl, qtok[:, 2 * j:2 * j + 2, :, 0:64],
                                   [bpkv[sl]], [bqtok[2 * j], bqtok[2 * j + 1]])
                          if not is_moba:
                              DVE.op(lambda: V.tensor_copy(out=qtok[:, 2 * j:2 * j + 2, :, 64:65],
                                                           in_=cum[:, gt0:gt0 + 2, fh0:fh0 + 2].unsqueeze(3)),
                                     [bcum], [bqtok[2 * j], bqtok[2 * j + 1]])

                      def q_stage2(j):
                          ncols = 64 if is_moba else 66
                          for hd in range(2):
                              for s2 in range(2):
                                  PE.op(lambda hd=hd, s2=s2: T.transpose(
                                      out=pQT[0:ncols, hd, s2 * 128:(s2 + 1) * 128],
                                      in_=qtok[:, 2 * j + s2, hd, 0:ncols], identity=identb[:]),
                                      [bqtok[2 * j + s2], bconst], [bpQT], signal=(hd == 1 and s2 == 1))
                          if j % 2 == 0:
                              ACT.op(lambda: A.copy(out=QTall[0:ncols, :, j * 256:(j + 1) * 256], in_=pQT[0:ncols, :, :]),
                                     [bpQT], [bQT[j]])
                          else:
                              DVE.op(lambda: V.tensor_copy(out=QTall[0:ncols, :, j * 256:(j + 1) * 256], in_=pQT[0:ncols, :, :]),
                                     [bpQT], [bQT[j]])

                      for jj in range(NQB + 2):
                          if jj < NQB:
                              q_stage1(jj)
                          if jj >= 2:
                              q_stage2(jj - 2)
                      cut(52)
                      fw.barrier()
                      cut(5)

                  with contextlib.ExitStack() as sA:
                      ps = [pm("ps%d" % i, [128, 512], stack=sA) for i in range(3)]
                      po = [pm("po%d" % i, [128, 512], stack=sA) for i in range(2)]
                      pot = pm("pot", [128, 4, 65], stack=sA)
                      pg = pm("pg", [128, 2, 4 * NB], stack=sA)
                      pQ2 = pm("pQ2", [AUG, 2, 256], BF16, stack=sA)
                      bps = [Buf() for _ in range(3)]; bpo = [Buf(), Buf()]; bpot = Buf(); _bpg = Buf(); bpg = [_bpg, _bpg]; bpQ2 = Buf()

                      def gate_a(j):
                          gs_ = j % 2
                          pgv = pg[:, gs_, :].rearrange("p (s h n) -> p s h n", s=2, h=2)
                          for hd in range(2):
                              for s2 in range(2):
                                  PE.op(lambda hd=hd, s2=s2: T.matmul(
                                      pgv[:, s2, hd, :], lhsT=QTall[0:64, hd, (2 * j + s2) * 128:(2 * j + s2 + 1) * 128],
                                      rhs=kmT[hd][:], start=True, stop=True),
                                      [bQT[j], bkmT[hd]], [bpg[gs_]], signal=(hd == 1 and s2 == 1))
                          gmj = gmaskb[:, j * NB:(j + 1) * NB].unsqueeze(1).to_broadcast([128, 4, NB])
                          pvj = pvb[:, j * NB:(j + 1) * NB].unsqueeze(1).to_broadcast([128, 4, NB])
                          ncj = ncb[:, j * NB:(j + 1) * NB].unsqueeze(1).to_broadcast([128, 4, NB])
                          DVE.op(lambda: V.tensor_tensor(out=gms[gs_][:], in0=pg[:, gs_, :].rearrange("p (a n) -> p a n", n=NB),
                                                         in1=gmj, op=ALU.add), [bpg[gs_], bconst], [bgms[gs_]])
                          for a in range(4):
                              DVE.op(lambda a=a: V.max(out=mx8[gs_][:, a, :], in_=gms[gs_][:, a, :]), [bgms[gs_]], [bmx8[gs_]])
                          for a in range(4):
                              DVE.op(lambda a=a: V.tensor_scalar(out=nsf[gs_][:, a, :], in0=gms[gs_][:, a, :],
                                                                 scalar1=mx8[gs_][:, a, 2:3], scalar2=NEG,
                                                                 op0=ALU.is_lt, op1=ALU.mult),
                                     [bgms[gs_], bmx8[gs_]], [bnsf[gs_]])
                          DVE.op(lambda: V.tensor_tensor(out=nsf[gs_][:], in0=nsf[gs_][:], in1=pvj, op=ALU.mult),
                                 [bnsf[gs_], bconst], [bnsf[gs_]])
                          DVE.op(lambda: V.tensor_tensor(out=qtok[:, 2 * j:2 * j + 2, :, 64:64 + NB],
                                                         in0=nsf[gs_][:].rearrange("p (s h) n -> p s h n", s=2),
                                                         in1=ncj.rearrange("p (s h) n -> p s h n", s=2), op=ALU.add),
                                 [bnsf[gs_], bconst], [bqtok[2 * j], bqtok[2 * j + 1]])

                      def gate_b(j):
                          for hd in range(2):
                              for s2 in range(2):
                                  PE.op(lambda hd=hd, s2=s2: T.transpose(
                                      out=pQ2[0:KA, hd, s2 * 128:(s2 + 1) * 128],
                                      in_=qtok[:, 2 * j + s2, hd, 0:KA], identity=identb[:]),
                                      [bqtok[2 * j + s2], bconst], [bpQ2], signal=(hd == 1 and s2 == 1))
                          DVE.op(lambda: V.tensor_copy(out=QTall[0:KA, :, j * 256:(j + 1) * 256], in_=pQ2[0:KA, :, :]),
                                 [bpQ2], [bQT[j]])

                      if is_moba:
                          gate_a(0); gate_a(1); gate_b(0); gate_b(1)
                      cnt = 0; ocnt = 0
                      pending = []

                      def epilogue(hd, jp, osl, hg):
                          DVE.op(lambda: V.tensor_copy(out=osb[osl][:], in_=po[osl][0:65, :]), [bpo[osl]], [bosb[osl]])
                          for s2 in range(4):
                              PE.op(lambda s2=s2: T.transpose(out=pot[:, s2, :], in_=osb[osl][:, s2 * 128:(s2 + 1) * 128],
                                                              identity=identf[0:65, 0:65]),
                                    [bosb[osl], bconst], [bpot], signal=(s2 == 3))
                          DVE.op(lambda: V.reciprocal(out=rden[:].unsqueeze(2), in_=pot[:, :, 64:65]), [bpot], [brden])
                          for s2 in range(4):
                              DVE.op(lambda s2=s2: V.tensor_scalar(out=Obuf[:, 2 * jp + s2, hg * 64:(hg + 1) * 64],
                                                                   in0=pot[:, s2, 0:64], scalar1=rden[:, s2:s2 + 1],
                                                                   scalar2=None, op0=ALU.mult),
                                     [bpot, brden], [bO[2 * jp + s2]])

                      for jp in range(0, NQB, 2):
                          for hd in range(2):
                              osl = ocnt % 2; ocnt += 1
                              hg = hbase + hd
                              nk0 = 4 * jp + 4; nk1 = 4 * jp + 8
                              q0 = jp * 256

                              def score(kt, c):
                                  k3 = c % 3
                                  lhsK = KTall[:, hd, kt * 128:(kt + 1) * 128]
                                  rd = [bKT[kt // 4], bKTaug, bQT[jp], bQT[jp + 1]]
                                  qa = QTall[:, hd, q0:q0 + 256]
                                  qb = QTall[:, hd, q0 + 256:q0 + 512]
                                  if kt < nk0 - 2:
                                      PE.op(lambda: T.matmul(ps[k3][:, 0:512], lhsT=lhsK, rhs=QTall[:, hd, q0:q0 + 512],
                                                             start=True, stop=True), rd, [bps[k3]], signal=True)
                                  elif kt < nk0:
                                      d = kt - (nk0 - 2)
                                      PE.op(lambda: T.matmul(ps[k3][:, 0:256], lhsT=lhsK, rhs=qa, start=True, stop=False),
                                            rd, [bps[k3]], signal=False)
                                      PE.op(lambda: T.matmul(ps[k3][:, 0:256], lhsT=identb[:], rhs=dmaskb[:, d, :],
                                                             start=False, stop=True), [bconst], [bps[k3]], signal=False)
                                      PE.op(lambda: T.matmul(ps[k3][:, 256:512], lhsT=lhsK, rhs=qb, start=True, stop=True),
                                            rd, [bps[k3]], signal=True)
                                  elif kt < nk1 - 2:
                                      PE.op(lambda: T.matmul(ps[k3][:, 256:512], lhsT=lhsK, rhs=qb, start=True, stop=True),
                                            rd, [bps[k3]], signal=True)
                                  else:
                                      d = kt - (nk1 - 2)
                                      PE.op(lambda: T.matmul(ps[k3][:, 256:512], lhsT=lhsK, rhs=qb, start=True, stop=False),
                                            rd, [bps[k3]], signal=False)
                                      PE.op(lambda: T.matmul(ps[k3][:, 256:512], lhsT=identb[:], rhs=dmaskb[:, d, :],
                                                             start=False, stop=True), [bconst], [bps[k3]], signal=True)

                              score(0, cnt)
                              score(1, cnt + 1)
                              for kt in range(nk1):
                                  c = cnt + kt; k3 = c % 3
                                  if kt + 2 < nk1:
                                      score(kt + 2, c + 2)
                                  lo = 0 if kt < nk0 else 256
                                  if is_moba:
                                      ACT.op(lambda: A.activation(out=pTs[k3][:, lo:512], in_=ps[k3][:, lo:512],
                                                                  func=AF.Exp, scale=0.125),
                                             [bps[k3]], [bpTs[k3]])
                                  else:
                                      ACT.op(lambda: A.activation(out=pTs[k3][:, lo:512], in_=ps[k3][:, lo:512],
                                                                  func=AF.Exp, scale=0.125,
                                                                  bias=negcum[:, kt, fh0 + hd:fh0 + hd + 1]),
                                             [bps[k3], bcum], [bpTs[k3]])
                                  vl = VApf[:, (kt * 2 + hd) * 80:(kt * 2 + hd) * 80 + 128]
                                  PE.op(lambda: T.matmul(po[osl][:, lo:512], lhsT=vl, rhs=pTs[k3][:, lo:512],
                                                         start=(kt == 0), stop=(kt == nk1 - 1)),
                                        [bVA[kt], bpTs[k3]], [bpo[osl]], signal=True)
                                  if kt == 1 and pending:
                                      pending.pop()()
                                  if is_moba and kt == 2 and jp + 2 < NQB:
                                      if hd == 0:
                                          gate_a(jp + 2); gate_a(jp + 3)
                                      else:
                                          gate_b(jp + 2); gate_b(jp + 3)
                              cnt += nk1
                              if pending:
                                  pending.pop()()
                              pending.append(lambda hd=hd, jp=jp, osl=osl, hg=hg: epilogue(hd, jp, osl, hg))
                      if pending:
                          pending.pop()()
                      fw.barrier()
                      cut(6)
              fw.barrier()
              cut(7)

          with contextlib.ExitStack() as sEF:
              h2T = sb("h2T", [128, KC, NOT * 128], BF16, stack=sEF)
              bh2T = [Buf() for _ in range(NOT)]
              stat2 = sb("stat2", [128, 16], stack=sEF)
              bst2 = [Buf() for _ in range(16)]
              with contextlib.ExitStack() as sE:
                  NSL = 4
                  woutb = sb("woutb", [128, KC, D], BF16, stack=sE)
                  xo = [sb("xo%d" % i, [128, D], stack=sE) for i in range(NSL)]
                  junkO = sb("junkO", [128, 2, Mw], stack=sE); junkX = sb("junkX", [128, D], stack=sE)
                  bjO = [Buf(), Buf()]; bjX = Buf()
                  mixb = [sb("mixb%d" % i, [128, D], BF16, stack=sE) for i in range(NSL)]
                  mixT = [sb("mixT%d" % i, [128, KC, 128], BF16, stack=sE) for i in range(NSL)]
                  x1t = [sb("x1t%d" % i, [128, D], stack=sE) for i in range(NSL)]
                  xn2 = [sb("xn2%d" % i, [128, D], stack=sE) for i in range(NSL)]
                  stE = sb("stE", [128, NSL, 8], stack=sE)
                  bsE = [[Buf() for _ in range(6)] for _ in range(NSL)]
                  pmt = [pm("pmt%d" % i, [128, KC * 128], BF16, stack=sE) for i in range(2)]
                  py = [pm("py%d" % i, [128, CW], stack=sE) for i in range(2)]
                  ptr2 = pm("ptr2", [128, KC * 128], BF16, stack=sE)
                  mbm = sb("mbm", [128, 2, D], stack=sE); bmbm = Buf()
                  hb2 = [sb("hb2%d" % i, [128, D], BF16, stack=sE) for i in range(NSL)]
                  bhb2 = [Buf() for _ in range(NSL)]
                  SP.dma(mbm[:, 0, :], modscr[0:1, 2 * D:3 * D].partition_broadcast(128), reads=[bmodscr], writes=[bmbm])
                  SP.dma(mbm[:, 1, :], modscr[0:1, 3 * D:4 * D].partition_broadcast(128), reads=[bmodscr], writes=[bmbm])
                  bwout = Buf(); bxo = [Buf() for _ in range(NSL)]; bmixb = [Buf() for _ in range(NSL)]
                  bmixT = [Buf() for _ in range(NSL)]; bx1t = [Buf() for _ in range(NSL)]; bxn2 = [Buf() for _ in range(NSL)]
                  bpmt = [Buf(), Buf()]; bpy = [Buf(), Buf()]; bptr2 = Buf()
                  POOL.dma(woutb[:], w_out_v[:, :, :], writes=[bwout])
                  ycn = [0]

                  def e_s1(i):
                      sl = i % NSL; j = i // 2; s_ = i % 2
                      gt = 4 * j + 2 + s_
                      ps_ = i % 2
                      SP.dma(xo[sl][:], xs[gt * 128:(gt + 1) * 128, :], writes=[bxo[sl]])
                      for g in range(2):
                          ACT.op(lambda g=g: A.activation(out=junkO[:, g, :], in_=Obuf[:, i, g * Mw:(g + 1) * Mw],
                                                          func=AF.Square, accum_out=stE[:, sl, g:g + 1]),
                                 [bO[i]], [bjO[g], bsE[sl][g]])
                      ACT.op(lambda: A.activation(out=stE[:, sl, 2:4], in_=stE[:, sl, 0:2], func=AF.Ln, scale=1.0 / Mw,
                                                  bias=epsb[:, 0:1]), [bsE[sl][0], bsE[sl][1], bepsb], [bsE[sl][2]])
                      ACT.op(lambda: A.activation(out=stE[:, sl, 2:4], in_=stE[:, sl, 2:4], func=AF.Exp, scale=-0.5),
                             [bsE[sl][2]], [bsE[sl][2]])
                      for g in range(2):
                          DVE.op(lambda g=g: V.scalar_tensor_tensor(
                              out=mixb[sl][:, g * Mw:(g + 1) * Mw], in0=Obuf[:, i, g * Mw:(g + 1) * Mw],
                              scalar=stE[:, sl, 2 + g:3 + g], in1=goutbc[:, g * Mw:(g + 1) * Mw],
                              op0=ALU.mult, op1=ALU.mult), [bO[i], bsE[sl][2], bconst], [bmixb[sl]])

                  def e_s1b(i):
                      sl = i % NSL
                      ps_ = i % 2
                      for kc in range(KC):
                          PE.op(lambda kc=kc: T.transpose(out=pmt[ps_][:, kc * 128:(kc + 1) * 128],
                                                          in_=mixb[sl][:, kc * 128:(kc + 1) * 128], identity=identb[:]),
                                [bmixb[sl], bconst], [bpmt[ps_]], signal=(kc == KC - 1))
                      ACT.op(lambda: A.copy(out=mixT[sl][:].rearrange("p k c -> p (k c)"), in_=pmt[ps_][:]),
                             [bpmt[ps_]], [bmixT[sl]])

                  def e_s2(i):
                      sl = i % NSL
                      for c2 in range(NC2):
                          ys = ycn[0] % 2; ycn[0] += 1
                          for kc in range(KC):
                              PE.op(lambda kc=kc: T.matmul(py[ys][:], lhsT=mixT[sl][:, kc, :],
                                                           rhs=woutb[:, kc, c2 * CW:(c2 + 1) * CW],
                                                           start=(kc == 0), stop=(kc == KC - 1)),
                                    [bmixT[sl], bwout], [bpy[ys]], signal=(kc == KC - 1))
                          DVE.op(lambda: V.tensor_tensor(out=x1t[sl][:, c2 * CW:(c2 + 1) * CW], in0=py[ys][:],
                                                         in1=gbc[:, 0, c2 * CW:(c2 + 1) * CW], op=ALU.mult),
                                 [bpy[ys], bgbc], [bx1t[sl]])
                          DVE.op(lambda: V.tensor_tensor(out=x1t[sl][:, c2 * CW:(c2 + 1) * CW],
                                                         in0=x1t[sl][:, c2 * CW:(c2 + 1) * CW],
                                                         in1=xo[sl][:, c2 * CW:(c2 + 1) * CW], op=ALU.add),
                                 [bx1t[sl], bxo[sl]], [bx1t[sl]])
                      SP.dma(out[i * 128:(i + 1) * 128, :], x1t[sl][:], reads=[bx1t[sl]], writes=[bout[i]])
                      ACT.op(lambda: A.activation(out=junkX[:], in_=x1t[sl][:], func=AF.Square,
                                                  accum_out=stE[:, sl, 4:5]), [bx1t[sl]], [bjX, bsE[sl][3]])
                      rstd_ops(stE[:, sl, 6:7], stE[:, sl, 4:5], D, bsE[sl][3], bsE[sl][5], stE[:, sl, 5:6], bsE[sl][4])
                      DVE.op(lambda: V.scalar_tensor_tensor(out=xn2[sl][:], in0=x1t[sl][:], scalar=stE[:, sl, 6:7],
                                                            in1=mbm[:, 1, :], op0=ALU.mult, op1=ALU.mult),
                             [bx1t[sl], bsE[sl][5], bmbm], [bxn2[sl]])
                      DVE.op(lambda: V.tensor_tensor(out=hb2[sl][:], in0=xn2[sl][:], in1=mbm[:, 0, :], op=ALU.add),
                             [bxn2[sl], bmbm], [bhb2[sl]])

                  def e_s3(i):
                      sl = i % NSL
                      for kc in range(KC):
                          PE.op(lambda kc=kc: T.transpose(out=ptr2[:, kc * 128:(kc + 1) * 128],
                                                          in_=hb2[sl][:, kc * 128:(kc + 1) * 128], identity=identb[:]),
                                [bhb2[sl], bconst], [bptr2], signal=(kc == KC - 1))
                      ACT.op(lambda: A.copy(out=h2T[:, :, i * 128:(i + 1) * 128],
                                            in_=ptr2[:].rearrange("p (k c) -> p k c", k=KC)),
                             [bptr2], [bh2T[i]])

                  for k in range(NOT + 3):
                      if k < NOT:
                          e_s1(k)
                      if 0 <= k - 1 < NOT:
                          e_s1b(k - 1)
                      if 0 <= k - 2 < NOT:
                          e_s2(k - 2)
                      if 0 <= k - 3 < NOT:
                          e_s3(k - 3)
                  fw.barrier()
                  cut(8)

              with contextlib.ExitStack() as sF:
                  HW_ = HT * 128
                  NG = HW_ // 512
                  NX = min(4, HT)
                  uT = sb("uT", [128, FC, HW_], BF16, stack=sF)
                  w1s = [sb("w1s%d" % i, [128, KC, 128], BF16, stack=sF) for i in range(8)]
                  w2s = [sb("w2s%d" % i, [128, 2, CW], BF16, stack=sF) for i in range(8)]
                  w2scr_v = w2scr.rearrange("(fc p) c -> p fc c", p=128)
                  relu = [sb("relu%d" % i, [128, 512], stack=sF) for i in range(2)]
                  x1r = [sb("x1r%d" % i, [128, CW], stack=sF) for i in range(NX)]
                  zo = [sb("zo%d" % i, [128, CW], stack=sF) for i in range(2)]
                  buT = [[Buf() for _ in range(NG)] for _ in range(FC)]
                  bw1 = [Buf() for _ in range(8)]; bw2 = [Buf() for _ in range(8)]
                  brelu = [Buf(), Buf()]; bx1r = [Buf() for _ in range(NX)]; bzo = [Buf(), Buf()]
                  w1c = 0; w2c = 0; rc = 0; zc = 0
                  for hh in range(2):
                      with contextlib.ExitStack() as s1:
                          pu = [pm("pu%d" % i, [128, 512], stack=s1) for i in range(4)]
                          bpu = [Buf() for _ in range(4)]
                          puc = 0
                          for fc in range(FC):
                              wsl = w1c % 8; w1c += 1
                              SP.dma(w1s[wsl][:].rearrange("p k c -> p (k c)"), w1scr[fc, :, :], reads=[bw1scr], writes=[bw1[wsl]])
                              for g in range(NG):
                                  pk = puc % 4; puc += 1
                                  tok0 = hh * HW_ + g * 512
                                  for kc in range(KC):
                                      PE.op(lambda kc=kc: T.matmul(pu[pk][:], lhsT=w1s[wsl][:, kc, :],
                                                                   rhs=h2T[:, kc, tok0:tok0 + 512],
                                                                   start=(kc == 0), stop=(kc == KC - 1)),
                                            [bw1[wsl]] + bh2T[tok0 // 128: tok0 // 128 + 4], [bpu[pk]],
                                            signal=(kc == KC - 1))
                                  rs = rc % 2; rc += 1
                                  ACT.op(lambda: A.activation(out=relu[rs][:], in_=pu[pk][:], func=AF.Relu),
                                         [bpu[pk]], [brelu[rs]])
                                  DVE.op(lambda: V.tensor_tensor(out=uT[:, fc, g * 512:(g + 1) * 512], in0=relu[rs][:],
                                                                 in1=relu[rs][:], op=ALU.mult),
                                         [brelu[rs]], [buT[fc][g]])
                          fw.barrier()
                          cut(9)
                      with contextlib.ExitStack() as s2:
                          acc = [pm("acc%d" % i, [128, CW], stack=s2) for i in range(HT)]
                          bacc = [Buf() for _ in range(HT)]
                          for c2 in range(NC2):
                              for tt in range(NX):
                                  i = hh * HT + tt
                                  ACT.dma(x1r[tt % NX][:], out[i * 128:(i + 1) * 128, c2 * CW:(c2 + 1) * CW],
                                          reads=[bout[i]], writes=[bx1r[tt % NX]])
                              for fg in range(FC // 2):
                                  wsl = w2c % 8; w2c += 1
                                  SP.dma(w2s[wsl][:], w2scr_v[:, fg * 2:(fg + 1) * 2, c2 * CW:(c2 + 1) * CW], reads=[bw2scr], writes=[bw2[wsl]])
                                  for fl in range(2):
                                      fc = fg * 2 + fl
                                      for tt in range(HT):
                                          PE.op(lambda fl=fl, tt=tt: T.matmul(acc[tt][:], lhsT=uT[:, fc, tt * 128:(tt + 1) * 128],
                                                                              rhs=w2s[wsl][:, fl, :],
                                                                              start=(fc == 0), stop=(fc == FC - 1)),
                                                [buT[fc][tt // 4], bw2[wsl]], [bacc[tt]],
                                                signal=(fc == FC - 1 or (fl == 1 and tt == HT - 1)))
                              for tt in range(HT):
                                  i = hh * HT + tt
                                  zs = zc % 2; zc += 1
                                  xs_ = tt % NX
                                  DVE.op(lambda: V.tensor_tensor(out=zo[zs][:], in0=acc[tt][:],
                                                                 in1=gbc[:, 1, c2 * CW:(c2 + 1) * CW], op=ALU.mult),
                                         [bacc[tt], bgbc], [bzo[zs]])
                                  DVE.op(lambda: V.tensor_tensor(out=zo[zs][:], in0=zo[zs][:], in1=x1r[xs_][:], op=ALU.add),
                                         [bzo[zs], bx1r[xs_]], [bzo[zs]])
                                  ACT.dma(out[i * 128:(i + 1) * 128, c2 * CW:(c2 + 1) * CW], zo[zs][:],
                                          reads=[bzo[zs]], writes=[bout[i]])
                                  if tt + NX < HT:
                                      i2 = hh * HT + tt + NX
                                      ACT.dma(x1r[xs_][:], out[i2 * 128:(i2 + 1) * 128, c2 * CW:(c2 + 1) * CW],
                                              reads=[bout[i2]], writes=[bx1r[xs_]])
                          fw.barrier()
                          cut(10)
                  fw.barrier()
      except _Stop:
        pass
      fw.stopped = False
      SP = fw.sp
      SP.drain_dmas()
      fw.act.drain_dmas()
      fw.pool.drain_dmas()
      fw.barrier()
    return nc


def _host_consts(cfg, h):
    S = cfg.S
    NT = S // 128; NB = S // 256; NQB = NB // 2
    NAUG = max(NB, 2)
    ident = np.eye(128, dtype=np.float32)
    ss, tt = np.meshgrid(np.arange(128), np.arange(128), indexing="ij")
    utri = (ss <= tt).astype(np.float32)
    kk = np.arange(128)[:, None, None]; dd = np.arange(2)[None, :, None]; qq = np.arange(256)[None, None, :]
    dmask = np.where(dd * 128 + kk > qq, NEG, 0.0).astype(np.float32).reshape(128, 512)
    pos = np.arange(S, dtype=np.float64) - (256.0 if h == 0 else 0.0)
    pos = np.maximum(pos, 0.0)
    inv_freq = ROPE_THETA ** (-np.arange(0, 16, 2, dtype=np.float64) / 16.0)
    ang = pos[:, None] * inv_freq[None, :]
    cs = np.concatenate([np.cos(ang), np.cos(ang), -np.sin(ang), np.sin(ang)], axis=1).astype(np.float32)
    cs = np.ascontiguousarray(cs.reshape(NT, 128, 32).transpose(1, 0, 2).reshape(128, NT * 32))
    onehot = np.zeros((NAUG, S), np.float32)
    for m in range(NB):
        onehot[m, m * 256:(m + 1) * 256] = 1.0
    foxaug = np.zeros((2, S), np.float32)
    foxaug[0, :] = 8.0
    if h == 0:
        foxaug[1, 0:256] = NEG
    gmask = np.full((NQB, NB), -1e30, np.float32)
    pastvalid = np.zeros((NQB, NB), np.float32)
    negconst = np.zeros((NQB, NB), np.float32)
    for j in range(NQB):
        P = 2 * j + 1
        for n in range(NB):
            if (1 - h) <= n < P:
                gmask[j, n] = 0.0
                pastvalid[j, n] = 1.0
        if h == 0:
            negconst[j, 0] = NEG
    return dict(ident=ident, utri=utri, dmask=dmask, cs=cs, onehot=onehot, foxaug=foxaug,
                gmask=gmask.reshape(1, -1), pastvalid=pastvalid.reshape(1, -1), negconst=negconst.reshape(1, -1))


def make_in_maps(cfg, x, c, w_ada, b_ada, w_in, b_forget, g_qn_moba, g_kn_moba, g_qn_fox, g_kn_fox,
                 g_out_moba, g_out_fox, w_out, w_ff1, w_ff2):
    B, S, D = x.shape
    KC = D // 128
    f = lambda a: np.ascontiguousarray(np.asarray(a, dtype=np.float32))
    shared = dict(w_ada=f(w_ada[0]), b_ada=f(b_ada[0]).reshape(1, -1), w_in=f(w_in[0]), w_out=f(w_out[0]),
                  w_ff1=f(w_ff1[0]), w_ff2=f(w_ff2[0]), b_forget=f(b_forget[0]).reshape(1, -1),
                  g_qn_moba=f(g_qn_moba[0]).reshape(1, -1), g_kn_moba=f(g_kn_moba[0]).reshape(1, -1),
                  g_qn_fox=f(g_qn_fox[0]).reshape(1, -1), g_kn_fox=f(g_kn_fox[0]).reshape(1, -1),
                  g_out=f(np.concatenate([np.asarray(g_out_moba[0]), np.asarray(g_out_fox[0])])).reshape(1, -1))
    consts = [_host_consts(cfg, 0), _host_consts(cfg, 1)]
    in_maps = []
    x = np.asarray(x, dtype=np.float32); c = np.asarray(c, dtype=np.float32)
    for b in range(B):
        for h in range(2):
            if h == 0:
                xs = np.concatenate([np.zeros((256, D), np.float32), x[b, :S - 256]], axis=0)
            else:
                xs = x[b]
            m = dict(shared)
            m.update(consts[h])
            m["xs"] = np.ascontiguousarray(xs)
            m["ccol"] = np.ascontiguousarray(c[b].reshape(KC, 128).T)
            in_maps.append(m)
    return in_maps


def gather(cfg, results, B):
    S, D = cfg.S, cfg.D
    NQB = S // 512
    y = np.zeros((B, S, D), np.float32)
    k = 0
    for b in range(B):
        for h in range(2):
            o = np.asarray(results[k]["out"]).reshape(NQB, 256, D)
            k += 1
            for j in range(NQB):
                blk = 2 * j + h
                y[b, blk * 256:(blk + 1) * 256, :] = o[j]
    return y


_NC_CACHE = {}


def kernel(x, c, w_ada, b_ada, w_in, b_forget, g_qn_moba, g_kn_moba, g_qn_fox, g_kn_fox,
           g_out_moba, g_out_fox, w_out, w_ff1, w_ff2):
    x = np.asarray(x)
    B, S, D = x.shape
    DFF = np.asarray(w_ff1).shape[-1]
    cfg = Cfg(D, S, DFF)
    key = (D, S, DFF)
    if key not in _NC_CACHE:
        _NC_CACHE[key] = build(cfg)
    nc = _NC_CACHE[key]
    in_maps = make_in_maps(cfg, x, c, w_ada, b_ada, w_in, b_forget, g_qn_moba, g_kn_moba, g_qn_fox, g_kn_fox,
                           g_out_moba, g_out_fox, w_out, w_ff1, w_ff2)
    res = run_bass_kernel_spmd(nc, in_maps, core_ids=list(range(2 * B)))
    return gather(cfg, res.results, B)
```

```python
import contextlib
import numpy as np
import concourse.bass as bass
import concourse.mybir as mybir
from concourse.bass_utils import run_bass_kernel_spmd

F32 = mybir.dt.float32
BF16 = mybir.dt.bfloat16
AF = mybir.ActivationFunctionType
ALU = mybir.AluOpType
AX = mybir.AxisListType
EPS = 1e-6
NEG = -30000.0
ROPE_THETA = 500000.0


class Tok:
    __slots__ = ("eng", "sem", "count")

    def __init__(self, eng, sem, count):
        self.eng = eng; self.sem = sem; self.count = count


class Buf:
    __slots__ = ("name", "w", "r")

    def __init__(self, name=""):
        self.name = name; self.w = None; self.r = {}


class Eng:
    def __init__(self, fw, eng, name, always_signal=True, ndma=8):
        self.fw = fw; self.eng = eng; self.name = name
        self.sem = fw.es.enter_context(fw.nc.semaphore("sem_" + name))
        self.n = 0
        self.waited = {}
        self.always = always_signal
        self.ndma = ndma
        self.dsems = []; self.dcnt = []; self.di = 0
        self.unsignaled = False

    def _wait_tok(self, tok):
        if tok is None:
            return
        if tok.eng is self and self.name == "pe":
            return
        key = id(tok.sem)
        if self.waited.get(key, 0) >= tok.count:
            return
        self.eng.wait_ge(tok.sem, tok.count)
        self.waited[key] = tok.count

    def _deps(self, reads, writes):
        for b in reads:
            self._wait_tok(b.w)
        for b in writes:
            self._wait_tok(b.w)
            for t in list(b.r.values()):
                self._wait_tok(t)

    def _record(self, tok, reads, writes):
        for b in reads:
            b.r[id(tok.sem)] = tok
        for b in writes:
            b.w = tok; b.r = {}

    def op(self, fn, reads=(), writes=(), signal=None):
        if self.fw.stopped:
            return None
        self._deps(reads, writes)
        ins = fn()
        sig = self.always if signal is None else signal
        if sig:
            self.n += 1
            ins.then_inc(self.sem, 1)
            tok = Tok(self, self.sem, self.n)
            self.unsignaled = False
        else:
            tok = Tok(self, self.sem, self.n + 1)
            self.unsignaled = True
        self._record(tok, reads, writes)
        return tok

    def dma(self, out, in_, reads=(), writes=(), **kw):
        if self.fw.stopped:
            return None
        self._deps(reads, writes)
        if len(self.dsems) < self.ndma:
            self.dsems.append(self.fw.es.enter_context(
                self.fw.nc.semaphore("dsem_%s_%d" % (self.name, len(self.dsems)))))
            self.dcnt.append(0)
        k = self.di % self.ndma
        self.di += 1
        sem = self.dsems[k]
        if self.dcnt[k] > 0:
            self._wait_tok(Tok(None, sem, self.dcnt[k]))
        self.dcnt[k] += 16
        self.eng.dma_start(out=out, in_=in_, **kw).then_inc(sem, 16)
        tok = Tok(None, sem, self.dcnt[k])
        self._record(tok, reads, writes)
        return tok

    def drain_dmas(self):
        for k, sem in enumerate(self.dsems):
            if self.dcnt[k] > 0:
                self._wait_tok(Tok(None, sem, self.dcnt[k]))


class FW:
    def __init__(self, nc, es):
        self.nc = nc; self.es = es
        self.stopped = False
        self.pe = Eng(self, nc.tensor, "pe", always_signal=False)
        self.act = Eng(self, nc.scalar, "act")
        self.dve = Eng(self, nc.vector, "dve")
        self.pool = Eng(self, nc.gpsimd, "pool")
        self.sp = Eng(self, nc.sync, "sp")
        self.engs = [self.pe, self.act, self.dve, self.pool, self.sp]

    def barrier(self):
        if self.stopped:
            return
        assert not self.pe.unsignaled, "PE has unsignaled trailing instructions"
        for e in self.engs:
            for f in self.engs:
                if f is not e and f.n > 0:
                    e._wait_tok(Tok(f, f.sem, f.n))
                for k, sem in enumerate(f.dsems):
                    if f.dcnt[k] > 0:
                        e._wait_tok(Tok(None, sem, f.dcnt[k]))


class _Stop(Exception):
    pass


class Cfg:
    stop = 0

    def __init__(self, D, S, DFF):
        self.D = D; self.S = S; self.DFF = DFF
        self.NM = D // 128; self.NF = D // 128


def build(cfg):
    D = cfg.D; S = cfg.S; DFF = cfg.DFF; NM = cfg.NM; NF = cfg.NF
    KC = D // 128; NT = S // 128; NB = S // 256; NQB = NB // 2; NOT = 2 * NQB
    Mw = 64 * NM; Fw = 64 * NF; INW = 3 * Mw + 3 * Fw + NF
    FC = DFF // 128
    NAUG = max(NB, 2); AUG = 64 + NAUG
    CW = min(512, D); NC2 = D // CW
    HT = NOT // 2
    assert Mw == Fw and NM % 2 == 0 and NF % 2 == 0 and NB >= 8 and (HT * 128) % 512 == 0

    nc = bass.Bass("TRN2", target_bir_lowering=False)

    def din(name, shape):
        return nc.dram_tensor(name, shape, F32, kind="ExternalInput").ap()

    xs = din("xs", [S, D]); ccol = din("ccol", [128, KC])
    w_ada = din("w_ada", [D, 6 * D]); b_ada = din("b_ada", [1, 6 * D])
    w_in = din("w_in", [D, INW]); w_out = din("w_out", [D, D])
    w_ff1 = din("w_ff1", [D, DFF]); w_ff2 = din("w_ff2", [DFF, D])
    bfg = din("b_forget", [1, NF])
    gqm = din("g_qn_moba", [1, 64]); gkm = din("g_kn_moba", [1, 64])
    gqf = din("g_qn_fox", [1, 64]); gkf = din("g_kn_fox", [1, 64])
    g_out = din("g_out", [1, D])
    ident = din("ident", [128, 128]); utri = din("utri", [128, 128])
    dmask = din("dmask", [128, 2 * 256]); cs = din("cs", [128, NT * 32])
    onehot = din("onehot", [NAUG, S]); foxaug = din("foxaug", [2, S])
    gmask = din("gmask", [1, NQB * NB]); pastvalid = din("pastvalid", [1, NQB * NB])
    negconst = din("negconst", [1, NQB * NB])
    out = nc.dram_tensor("out", [NOT * 128, D], F32, kind="ExternalOutput").ap()
    modscr = nc.dram_tensor("modscr", [1, 4 * D], F32, kind="Internal").ap()
    w1scr = nc.dram_tensor("w1scr", [DFF // 128, 128, D], BF16, kind="Internal").ap()
    w2scr = nc.dram_tensor("w2scr", [DFF, D], BF16, kind="Internal").ap()

    w_ada_v = w_ada.rearrange("(kc p) c -> p kc c", p=128)
    w_in_v = w_in.rearrange("(kc p) c -> p kc c", p=128)
    w_out_v = w_out.rearrange("(kc p) c -> p kc c", p=128)
    w_ff1_v = w_ff1.rearrange("(kc p) c -> p kc c", p=128)
    w_ff2_v = w_ff2.rearrange("(fc p) c -> p fc c", p=128)

    def cut(k):
        if cfg.stop == k:
            fw.stopped = True

    with contextlib.ExitStack() as es:
      fw = FW(nc, es)
      try:
          PE, ACT, DVE, POOL, SP = fw.pe, fw.act, fw.dve, fw.pool, fw.sp
          T = nc.tensor; A = nc.scalar; V = nc.vector; G = nc.gpsimd

          uniq = [0]

          def sb(name, shape, dt=F32, stack=es):
              uniq[0] += 1
              return stack.enter_context(nc.sbuf_tensor("%s_%d" % (name, uniq[0]), shape, dt))

          def pm(name, shape, dt=F32, stack=es):
              uniq[0] += 1
              esz = 4 if dt == F32 else 2
              nfree = 1
              for d_ in shape[1:]:
                  nfree *= d_
              nel = ((nfree * esz + 2047) // 2048) * 2048 // esz
              t_ = stack.enter_context(nc.psum_tensor("%s_%d" % (name, uniq[0]), [128, nel], dt))
              ap = t_[0:shape[0], 0:nfree]
              if len(shape) == 3:
                  ap = ap.rearrange("p (a b) -> p a b", a=shape[1])
              return ap

          def rstd_ops(dst, src, n, bsrc, bdst, tmp, btmp):
              ACT.op(lambda: A.activation(out=tmp, in_=src, func=AF.Ln, scale=1.0 / n, bias=epsb[:, 0:1]),
                     [bsrc, bepsb], [btmp])
              ACT.op(lambda: A.activation(out=dst, in_=tmp, func=AF.Exp, scale=-0.5), [btmp], [bdst])

          identf = sb("identf", [128, 128]); identb = sb("identb", [128, 128], BF16)
          utrif = sb("utrif", [128, 128]); onesf = sb("onesf", [128, 128])
          epsb = sb("epsb", [128, 1])
          dmaskb = sb("dmaskb", [128, 2, 256], BF16)
          cst = sb("cst", [128, NT, 32])
          gqm_b = sb("gqm_b", [128, 4, 64]); gkm_b = sb("gkm_b", [128, 4, 64])
          gqf_b = sb("gqf_b", [128, 4, 64]); gkf_b = sb("gkf_b", [128, 4, 64])
          goutbc = sb("goutbc", [128, D]); gbc = sb("gbc", [128, 2, D])
          modcol = sb("modcol", [128, 4 * KC])
          cum = sb("cum", [128, NT, NF]); negcum = sb("negcum", [128, NT, NF])
          gmaskb = sb("gmaskb", [128, NQB * NB]); pvb = sb("pvb", [128, NQB * NB]); ncb = sb("ncb", [128, NQB * NB])
          bfb = sb("bfb", [128, NF])
          Obuf = sb("Obuf", [128, NOT, D], BF16)
          bconst = Buf("const"); bepsb = Buf("eps"); bmodcol = Buf("modcol"); bmodscr = Buf("modscr"); bgbc = Buf("gbc"); bcum = Buf("cum")
          bO = [Buf("O%d" % i) for i in range(NOT)]
          bout = [Buf("out%d" % i) for i in range(NOT)]

          SP.dma(identf[:], ident[:, :], writes=[bconst])
          SP.dma(utrif[:], utri[:, :], writes=[bconst])
          SP.dma(cst[:].rearrange("p t c -> p (t c)"), cs[:, :], writes=[bconst])
          for gt_, gsrc in ((gqm_b, gqm), (gkm_b, gkm), (gqf_b, gqf), (gkf_b, gkf)):
              for hd in range(4):
                  SP.dma(gt_[:, hd, :], gsrc.partition_broadcast(128), writes=[bconst])
          SP.dma(goutbc[:], g_out.partition_broadcast(128), writes=[bconst])
          SP.dma(gmaskb[:], gmask.partition_broadcast(128), writes=[bconst])
          SP.dma(pvb[:], pastvalid.partition_broadcast(128), writes=[bconst])
          SP.dma(ncb[:], negconst.partition_broadcast(128), writes=[bconst])
          SP.dma(bfb[:], bfg.partition_broadcast(128), writes=[bconst])
          POOL.dma(identb[:], ident[:, :], writes=[bconst])
          POOL.dma(dmaskb[:].rearrange("p a q -> p (a q)"), dmask[:, :], writes=[bconst])
          DVE.op(lambda: V.memset(onesf[:], 1.0), [], [bconst])
          DVE.op(lambda: V.memset(epsb[:], EPS), [], [bepsb])
          cut(1)

          with contextlib.ExitStack() as sa:
              csil = sb("csil", [128, KC], stack=sa); scl = sb("scl", [128, KC], stack=sa)
              lbc = sb("lbc", [128, KC, 128], stack=sa)
              wada = [sb("wada%d" % i, [128, KC, 512], stack=sa) for i in range(4)]
              modbc = sb("modbc", [128, 6 * D], stack=sa)
              pmod = [pm("pmod%d" % i, [128, 512], stack=sa) for i in range(4)]
              pcol = pm("pcol", [128, 4 * KC], stack=sa)
              bcs = Buf(); bscl = Buf(); blbc = Buf(); bmod = Buf(); bpcol = Buf()
              bwada = [Buf() for _ in range(4)]; bpmod = [Buf() for _ in range(4)]
              SP.dma(csil[:], ccol[:, :], writes=[bcs])
              SP.dma(modbc[:], b_ada.partition_broadcast(128), writes=[bmod])
              ACT.op(lambda: A.activation(out=scl[:], in_=csil[:], func=AF.Silu), [bcs], [bscl])
              for kc in range(KC):
                  DVE.op(lambda kc=kc: V.tensor_copy(out=lbc[:, kc, :], in_=scl[:, kc:kc + 1].to_broadcast([128, 128])),
                         [bscl], [blbc])
              NCH = 6 * D // 512
              for ch in range(NCH):
                  sl = ch % 4
                  (SP if ch % 2 == 0 else ACT).dma(wada[sl][:], w_ada_v[:, :, ch * 512:(ch + 1) * 512], writes=[bwada[sl]])
                  for kc in range(KC):
                      PE.op(lambda kc=kc: T.matmul(pmod[sl][:], lhsT=lbc[:, kc, :], rhs=wada[sl][:, kc, :],
                                                   start=(kc == 0), stop=(kc == KC - 1)),
                            [blbc, bwada[sl]], [bpmod[sl]], signal=(kc == KC - 1))
                  DVE.op(lambda: V.tensor_tensor(out=modbc[:, ch * 512:(ch + 1) * 512], in0=pmod[sl][:],
                                                 in1=modbc[:, ch * 512:(ch + 1) * 512], op=ALU.add),
                         [bpmod[sl], bmod], [bmod])
              for vi, v in enumerate([0, 1, 3, 4]):
                  if v in (1, 4):
                      DVE.op(lambda v=v: V.tensor_scalar_add(out=modbc[0:1, v * D:(v + 1) * D], in0=modbc[0:1, v * D:(v + 1) * D],
                                                             scalar1=1.0), [bmod], [bmod])
                  SP.dma(modscr[0:1, vi * D:(vi + 1) * D], modbc[0:1, v * D:(v + 1) * D], reads=[bmod], writes=[bmodscr])
              DVE.op(lambda: V.tensor_copy(out=gbc[:, 0, :], in_=modbc[:, 2 * D:3 * D]), [bmod], [bgbc])
              DVE.op(lambda: V.tensor_copy(out=gbc[:, 1, :], in_=modbc[:, 5 * D:6 * D]), [bmod], [bgbc])
              fw.barrier()
              cut(2)

          with contextlib.ExitStack() as sbd:
              hT = sb("hT", [128, KC, S], BF16, stack=sbd)
              bhT = [Buf("hT%d" % t) for t in range(NT)]
              stat = sb("stat", [128, 16], stack=sbd)
              bstat = [Buf() for _ in range(16)]

              with contextlib.ExitStack() as sB:
                  NB_ = 6
                  junk = sb("junk", [128, D], stack=sB); bjunk = Buf()
                  mba = sb("mba", [128, 2, D], stack=sB); bmba = Buf()
                  xt = [sb("xt%d" % i, [128, D], stack=sB) for i in range(NB_)]
                  xn = [sb("xn%d" % i, [128, D], stack=sB) for i in range(NB_)]
                  hb = [sb("hb%d" % i, [128, D], BF16, stack=sB) for i in range(NB_)]
                  stB = sb("stB", [128, NB_, 4], stack=sB)
                  bsB = [[Buf() for _ in range(3)] for _ in range(NB_)]
                  ptr = [pm("ptr%d" % i, [128, KC * 128], BF16, stack=sB) for i in range(2)]
                  bxt = [Buf() for _ in range(NB_)]; bxn = [Buf() for _ in range(NB_)]; bhb = [Buf() for _ in range(NB_)]
                  bptr = [Buf(), Buf()]
                  SP.dma(mba[:, 0, :], modscr[0:1, 0:D].partition_broadcast(128), reads=[bmodscr], writes=[bmba])
                  SP.dma(mba[:, 1, :], modscr[0:1, D:2 * D].partition_broadcast(128), reads=[bmodscr], writes=[bmba])

                  def b_s1(t):
                      sl = t % NB_
                      SP.dma(xt[sl][:], xs[t * 128:(t + 1) * 128, :], writes=[bxt[sl]])
                      ACT.op(lambda: A.activation(out=junk[:], in_=xt[sl][:], func=AF.Square,
                                                  accum_out=stB[:, sl, 0:1]), [bxt[sl]], [bjunk, bsB[sl][0]])

                  def b_s2(t):
                      sl = t % NB_
                      rstd_ops(stB[:, sl, 2:3], stB[:, sl, 0:1], D, bsB[sl][0], bsB[sl][2], stB[:, sl, 1:2], bsB[sl][1])
                      DVE.op(lambda: V.scalar_tensor_tensor(out=xn[sl][:], in0=xt[sl][:], scalar=stB[:, sl, 2:3],
                                                            in1=mba[:, 1, :], op0=ALU.mult, op1=ALU.mult),
                             [bxt[sl], bsB[sl][2], bmba], [bxn[sl]])
                      DVE.op(lambda: V.tensor_tensor(out=hb[sl][:], in0=xn[sl][:], in1=mba[:, 0, :], op=ALU.add),
                             [bxn[sl], bmba], [bhb[sl]])

                  def b_s3(t):
                      sl = t % NB_; ps_ = t % 2
                      for kc in range(KC):
                          PE.op(lambda kc=kc: T.transpose(out=ptr[ps_][:, kc * 128:(kc + 1) * 128],
                                                          in_=hb[sl][:, kc * 128:(kc + 1) * 128], identity=identb[:]),
                                [bhb[sl], bconst], [bptr[ps_]], signal=(kc == KC - 1))
                      ACT.op(lambda: A.copy(out=hT[:, :, t * 128:(t + 1) * 128],
                                            in_=ptr[ps_][:].rearrange("p (k c) -> p k c", k=KC)),
                             [bptr[ps_]], [bhT[t]])

                  for k in range(NT + 4):
                      if k < NT:
                          b_s1(k)
                      if 0 <= k - 3 < NT:
                          b_s2(k - 3)
                      if 0 <= k - 4 < NT:
                          b_s3(k - 4)
                  fw.barrier()
                  cut(3)

              with contextlib.ExitStack() as sC:
                  NN = NT * NF
                  wf = sb("wf", [128, KC, NF], BF16, stack=sC)
                  zf = sb("zf", [128, NN], stack=sC); ef = sb("ef", [128, NN], stack=sC)
                  lf = sb("lf", [128, NN], stack=sC); Tsb = sb("Tsb", [128, NN], stack=sC)
                  carry = sb("carry", [128, NT, NF], stack=sC)
                  pfl = pm("pfl", [128, NN], stack=sC); pW = pm("pW", [128, NN], stack=sC); pT = pm("pT", [128, NN], stack=sC)
                  bwf = Buf(); bzf = Buf(); bef = Buf(); blf = Buf(); bTsb = Buf(); bcarry = Buf()
                  bpfl = Buf(); bpW = Buf(); bpT = Buf()
                  POOL.dma(wf[:], w_in_v[:, :, INW - NF:INW], writes=[bwf])
                  for t in range(NT):
                      for kc in range(KC):
                          PE.op(lambda t=t, kc=kc: T.matmul(pfl[:, t * NF:(t + 1) * NF],
                                                            lhsT=hT[:, kc, t * 128:(t + 1) * 128], rhs=wf[:, kc, :],
                                                            start=(kc == 0), stop=(kc == KC - 1)),
                                [bhT[t], bwf], [bpfl], signal=(t == NT - 1 and kc == KC - 1))
                  DVE.op(lambda: V.tensor_tensor(out=zf[:].rearrange("p (t f) -> p t f", f=NF),
                                                 in0=pfl[:].rearrange("p (t f) -> p t f", f=NF),
                                                 in1=bfb[:].unsqueeze(1).to_broadcast([128, NT, NF]), op=ALU.add),
                         [bpfl, bconst], [bzf])
                  ACT.op(lambda: A.activation(out=ef[:], in_=zf[:], func=AF.Exp, scale=-1.0), [bzf], [bef])
                  ACT.op(lambda: A.activation(out=zf[:], in_=ef[:], func=AF.Ln, scale=1.0, bias=onesf[:, 0:1]),
                         [bef, bconst], [bzf])
                  DVE.op(lambda: V.tensor_scalar_mul(out=lf[:], in0=zf[:], scalar1=-1.0), [bzf], [blf])
                  PE.op(lambda: T.matmul(pW[:], lhsT=utrif[:], rhs=lf[:], start=True, stop=True), [blf, bconst], [bpW], signal=True)
                  PE.op(lambda: T.matmul(pT[:], lhsT=onesf[:], rhs=lf[:], start=True, stop=True), [blf, bconst], [bpT], signal=True)
                  DVE.op(lambda: V.tensor_copy(out=Tsb[:], in_=pT[:]), [bpT], [bTsb])
                  DVE.op(lambda: V.memset(carry[:, 0, :], 0.0), [], [bcarry])
                  for i in range(1, NT):
                      DVE.op(lambda i=i: V.tensor_tensor(out=carry[:, i, :], in0=carry[:, i - 1, :],
                                                         in1=Tsb[:, (i - 1) * NF:i * NF], op=ALU.add),
                             [bcarry, bTsb], [bcarry])
                  DVE.op(lambda: V.tensor_tensor(out=cum[:].rearrange("p t f -> p (t f)"), in0=pW[:],
                                                 in1=carry[:].rearrange("p t f -> p (t f)"), op=ALU.add),
                         [bpW, bcarry], [bcum])
                  DVE.op(lambda: V.tensor_scalar_mul(out=negcum[:].rearrange("p t f -> p (t f)"),
                                                     in0=cum[:].rearrange("p t f -> p (t f)"), scalar1=-1.0),
                         [bcum], [bcum])
                  fw.barrier()
                  cut(4)

              KTall = sb("KTall", [128, 2, S], BF16, stack=sbd)
              QTall = sb("QTall", [128, 2, NOT * 128], BF16, stack=sbd)
              VAp = sb("VAp", [128, NT + 1, 2, 80], BF16, stack=sbd)
              qtok = sb("qtok", [128, NOT, 2, AUG], BF16, stack=sbd)
              wq = [sb("wq%d" % i, [128, KC, 128], BF16, stack=sbd) for i in range(2)]
              wkv = [sb("wkv%d" % i, [128, KC, 256], BF16, stack=sbd) for i in range(2)]
              NS = 4; LAG = 2
              knf = [sb("knf%d" % i, [128, 4, 64], stack=sbd) for i in range(NS)]
              kb = [sb("kb%d" % i, [128, 4, 64], BF16, stack=sbd) for i in range(NS)]
              rp = [sb("rp%d" % i, [128, 2, 4, 16], stack=sbd) for i in range(NS)]
              statp = sb("statp", [128, NS, 12], stack=sbd)
              junkp = sb("junkp", [128, NS, 256], stack=sbd)
              bstp = [[Buf() for _ in range(3)] for _ in range(NS)]
              bjunkp = [Buf() for _ in range(NS)]
              bjunkq = [[Buf(), Buf()] for _ in range(NS)]
              bstq = [[Buf(), Buf()] for _ in range(NS)]
              CgK = sb("CgK", [128, NT, 16], stack=sbd); SgK = sb("SgK", [128, NT, 16], stack=sbd)
              CgQ = sb("CgQ", [128, NT, 16], stack=sbd); SgQ = sb("SgQ", [128, NT, 16], stack=sbd)
              brt = Buf()
              for Cg_, Sg_, gb_ in ((CgK, SgK, gkm_b), (CgQ, SgQ, gqm_b)):
                  DVE.op(lambda: V.tensor_tensor(out=Cg_[:], in0=cst[:, :, 0:16],
                                                 in1=gb_[:, 0, 0:16].unsqueeze(1).to_broadcast([128, NT, 16]), op=ALU.mult),
                         [bconst], [brt])
                  DVE.op(lambda: V.tensor_tensor(out=Sg_[:, :, 0:8], in0=cst[:, :, 16:24],
                                                 in1=gb_[:, 0, 8:16].unsqueeze(1).to_broadcast([128, NT, 8]), op=ALU.mult),
                         [bconst], [brt])
                  DVE.op(lambda: V.tensor_tensor(out=Sg_[:, :, 8:16], in0=cst[:, :, 24:32],
                                                 in1=gb_[:, 0, 0:8].unsqueeze(1).to_broadcast([128, NT, 8]), op=ALU.mult),
                         [bconst], [brt])
              kms = [sb("kms%d" % i, [64, NB], stack=sbd) for i in range(2)]
              kmT = [sb("kmT%d" % i, [64, NB], BF16, stack=sbd) for i in range(2)]
              gms = [sb("gms%d" % i, [128, 4, NB], stack=sbd) for i in range(2)]; mx8 = [sb("mx8%d" % i, [128, 4, 8], stack=sbd) for i in range(2)]
              nsf = [sb("nsf%d" % i, [128, 4, NB], stack=sbd) for i in range(2)]
              pTs = [sb("pTs%d" % i, [128, 512], BF16, stack=sbd) for i in range(3)]
              osb = [sb("osb%d" % i, [65, 512], stack=sbd) for i in range(2)]
              rden = sb("rden", [128, 4], stack=sbd)
              bKT = [Buf() for _ in range(NT // 4)]
              bKTaug = Buf()
              bQT = [Buf() for _ in range(NQB)]
              bVA = [Buf() for _ in range(NT)]
              bqtok = [Buf() for _ in range(NOT)]
              bwq = [Buf(), Buf()]; bwkv = [Buf(), Buf()]
              bknf = [Buf() for _ in range(NS)]; bkb = [Buf() for _ in range(NS)]; brp = [Buf() for _ in range(NS)]
              bkms = [Buf(), Buf()]; bkmT = [Buf(), Buf()]; bgms = [Buf(), Buf()]; bmx8 = [Buf(), Buf()]; bnsf = [Buf(), Buf()]
              bpTs = [Buf() for _ in range(3)]; bosb = [Buf(), Buf()]; brden = Buf()
              DVE.op(lambda: V.memset(VAp[:].rearrange("p t h c -> p (t h c)"), 1.0), [], bVA)
              VApf = VAp[:].rearrange("p t h c -> p (t h c)")
              DVE.op(lambda: V.memset(KTall[64:128, :, :].rearrange("p h s -> p (h s)"), 0.0), [], [bKTaug])
              DVE.op(lambda: V.memset(QTall[64:128, :, :].rearrange("p h s -> p (h s)"), 0.0), [], bQT)
              DVE.op(lambda: V.memset(qtok[:].rearrange("p t h c -> p (t h c)"), 1.0), [], bqtok)

              def v4(ap3):
                  return ap3.rearrange("p (a h) c -> p a h c", a=2)

              def normrope(src4, gain_b, Cg, Sg, is_moba, t0, sl, dst4, reads, writes):
                  ACT.op(lambda: A.activation(out=v4(junkp[:, sl, :].rearrange("p (a c) -> p a c", c=64)), in_=src4,
                                              func=AF.Square), reads, [bjunkp[sl]])
                  DVE.op(lambda: V.tensor_reduce(out=statp[:, sl, 0:4], in_=junkp[:, sl, :].rearrange("p (a c) -> p a c", c=64),
                                                 axis=AX.X, op=ALU.add), [bjunkp[sl]], [bstp[sl][0]])
                  rstd_ops(statp[:, sl, 8:12], statp[:, sl, 0:4], 64, bstp[sl][0], bstp[sl][2], statp[:, sl, 4:8], bstp[sl][1])
                  DVE.op(lambda: V.tensor_tensor(out=v4(knf[sl][:]), in0=src4,
                                                 in1=v4(statp[:, sl, 8:12].unsqueeze(2).to_broadcast([128, 4, 64])),
                                                 op=ALU.mult), reads + [bstp[sl][2]], [bknf[sl]])
                  if not is_moba:
                      DVE.op(lambda: V.tensor_tensor(out=dst4, in0=v4(knf[sl][:]), in1=v4(gain_b[:]), op=ALU.mult),
                             [bknf[sl], bconst], writes)
                      return
                  DVE.op(lambda: V.tensor_tensor(out=dst4[:, :, :, 16:64], in0=v4(knf[sl][:, :, 16:64]),
                                                 in1=v4(gain_b[:, :, 16:64]), op=ALU.mult), [bknf[sl], bconst], writes)
                  r = rp[sl]
                  k4 = v4(knf[sl][:])
                  POOL.op(lambda: G.tensor_tensor(out=v4(r[:, 0, :, :]), in0=k4[:, :, :, 0:16],
                                                  in1=Cg[:, t0:t0 + 2, :].unsqueeze(2).to_broadcast([128, 2, 2, 16]), op=ALU.mult),
                          [bknf[sl], brt], [brp[sl]])
                  POOL.op(lambda: G.tensor_tensor(out=v4(r[:, 1, :, 0:8]), in0=k4[:, :, :, 8:16],
                                                  in1=Sg[:, t0:t0 + 2, 0:8].unsqueeze(2).to_broadcast([128, 2, 2, 8]), op=ALU.mult),
                          [bknf[sl], brt], [brp[sl]])
                  POOL.op(lambda: G.tensor_tensor(out=v4(r[:, 1, :, 8:16]), in0=k4[:, :, :, 0:8],
                                                  in1=Sg[:, t0:t0 + 2, 8:16].unsqueeze(2).to_broadcast([128, 2, 2, 8]), op=ALU.mult),
                          [bknf[sl], brt], [brp[sl]])
                  POOL.op(lambda: G.tensor_tensor(out=dst4[:, :, :, 0:16], in0=v4(r[:, 0, :, :]), in1=v4(r[:, 1, :, :]), op=ALU.add),
                          [brp[sl]], writes)

              n_pairs = NM // 2 + NF // 2
              bw1scr = Buf("w1scr"); bw2scr = Buf("w2scr")
              bg_jobs = []
              w1scr_pv = w1scr.rearrange("f p x -> p f x")
              FQ = 8 if FC % 8 == 0 else FC
              for kc_ in range(KC):
                  for f0_ in range(0, FC, FQ):
                      bg_jobs.append(lambda kc_=kc_, f0_=f0_: POOL.dma(
                          w1scr_pv[:, f0_:f0_ + FQ, kc_ * 128:(kc_ + 1) * 128],
                          w_ff1[kc_ * 128:(kc_ + 1) * 128, f0_ * 128:(f0_ + FQ) * 128].rearrange("p (f c) -> p f c", c=128),
                          writes=[bw1scr]))
              for r_ in range(DFF // 512):
                  bg_jobs.append(lambda r_=r_: POOL.dma(w2scr[r_ * 512:(r_ + 1) * 512, :], w_ff2[r_ * 512:(r_ + 1) * 512, :],
                                                        writes=[bw2scr]))
              for u in range(n_pairs):
                  is_moba = u < NM // 2
                  if is_moba:
                      hbase = 2 * u
                      qc0 = hbase * 64; kc0 = Mw + hbase * 64; vc0 = 2 * Mw + hbase * 64
                      gq_b, gk_b = gqm_b, gkm_b
                      KA = 64 + NB
                  else:
                      fh0 = 2 * (u - NM // 2)
                      hbase = NM + fh0
                      qc0 = 3 * Mw + fh0 * 64; kc0 = 3 * Mw + Fw + fh0 * 64; vc0 = 3 * Mw + 2 * Fw + fh0 * 64
                      gq_b, gk_b = gqf_b, gkf_b
                      KA = 66
                  ws = u % 2
                  POOL.dma(wq[ws][:], w_in_v[:, :, qc0:qc0 + 128], writes=[bwq[ws]])
                  POOL.dma(wkv[ws][:, :, 0:128], w_in_v[:, :, kc0:kc0 + 128], writes=[bwkv[ws]])
                  POOL.dma(wkv[ws][:, :, 128:256], w_in_v[:, :, vc0:vc0 + 128], writes=[bwkv[ws]])
                  if not is_moba:
                      nbg = (len(bg_jobs) + n_pairs - 1 - u) // (n_pairs - u) if bg_jobs else 0
                      for _ in range(nbg):
                          bg_jobs.pop(0)()
                  if u == 0:
                      for hd in range(2):
                          POOL.dma(KTall[64:64 + NAUG, hd, :], onehot[:, :], writes=[bKTaug])
                  if u == NM // 2:
                      DVE.op(lambda: V.memset(KTall[64:128, :, :].rearrange("p h s -> p (h s)"), 0.0), [], [bKTaug])
                      for hd in range(2):
                          POOL.dma(KTall[64:66, hd, :], foxaug[:, :], writes=[bKTaug])
                      DVE.op(lambda: V.memset(qtok[:].rearrange("p t h c -> p (t h c)"), 1.0), [], bqtok)

                  with contextlib.ExitStack() as sP:
                      pkv = [pm("pkv%d" % i, [128, 2, 256], stack=sP) for i in range(NS)]
                      pKT = [pm("pKT%d" % i, [64, 2, 512], BF16, stack=sP) for i in range(2)]
                      pQT = pm("pQT", [AUG, 2, 256], BF16, stack=sP)
                      bpkv = [Buf() for _ in range(NS)]; bpKT = [Buf(), Buf()]; bpQT = Buf()
                      Cgk, Sgk, Cgq, Sgq = CgK, SgK, CgQ, SgQ

                      def kv_stage1(tp):
                          sl = tp % NS; t0 = 2 * tp
                          for a in range(2):
                              t = t0 + a
                              for kc in range(KC):
                                  PE.op(lambda kc=kc: T.matmul(pkv[sl][:, a, :], lhsT=hT[:, kc, t * 128:(t + 1) * 128],
                                                               rhs=wkv[ws][:, kc, :], start=(kc == 0), stop=(kc == KC - 1)),
                                        [bhT[t], bwkv[ws]], [bpkv[sl]], signal=(a == 1 and kc == KC - 1))
                          vsrc = pkv[sl][:, :, 128:256].rearrange("p a (h c) -> p a h c", h=2)
                          ksrc = pkv[sl][:, :, 0:128].rearrange("p a (h c) -> p a h c", h=2)
                          if is_moba:
                              ACT.op(lambda: A.copy(out=VAp[:, t0:t0 + 2, :, 0:64], in_=vsrc),
                                     [bpkv[sl]], [bVA[t0], bVA[t0 + 1]])
                          normrope(ksrc, gk_b, Cgk, Sgk, is_moba, t0, sl, v4(kb[sl][:]), [bpkv[sl]], [bkb[sl]])
                          if not is_moba:
                              DVE.op(lambda: V.tensor_copy(out=VAp[:, t0:t0 + 2, :, 0:64], in_=vsrc),
                                     [bpkv[sl]], [bVA[t0], bVA[t0 + 1]])

                      def kv_stage2(tp):
                          sl = tp % NS; t0 = 2 * tp
                          g4 = t0 // 4; gs = g4 % 2
                          for a in range(2):
                              t = t0 + a
                              for hd in range(2):
                                  PE.op(lambda hd=hd: T.transpose(out=pKT[gs][:, hd, (t % 4) * 128:(t % 4 + 1) * 128],
                                                                  in_=kb[sl][:, a * 2 + hd, :], identity=identb[:]),
                                        [bkb[sl], bconst], [bpKT[gs]], signal=(t % 4 == 3 and hd == 1))
                          if (t0 + 1) % 4 == 3:
                              if g4 % 2 == 0:
                                  ACT.op(lambda: A.copy(out=KTall[0:64, :, g4 * 512:(g4 + 1) * 512], in_=pKT[gs][:, :, :]),
                                         [bpKT[gs]], [bKT[g4]])
                              else:
                                  DVE.op(lambda: V.tensor_copy(out=KTall[0:64, :, g4 * 512:(g4 + 1) * 512], in_=pKT[gs][:, :, :]),
                                         [bpKT[gs]], [bKT[g4]])

                      NTP = NT // 2
                      for tt in range(NTP + 2):
                          if tt < NTP:
                              kv_stage1(tt)
                          if tt >= 2:
                              kv_stage2(tt - 2)
                      cut(51)

                      if is_moba:
                          for hd in range(2):
                              DVE.op(lambda hd=hd: V.tensor_reduce(out=kms[hd][:], in_=KTall[0:64, hd, :].rearrange("p (n l) -> p n l", l=256),
                                                                   axis=AX.X, op=ALU.add), bKT, [bkms[hd]])
                              DVE.op(lambda hd=hd: V.tensor_scalar_mul(out=kmT[hd][:], in0=kms[hd][:], scalar1=1.0 / 256.0),
                                     [bkms[hd]], [bkmT[hd]])

                      def q_stage1(j):
                          sl = j % NS
                          gt0 = 4 * j + 2
                          for a in range(2):
                              gt = gt0 + a
                              for kc in range(KC):
                                  PE.op(lambda kc=kc: T.matmul(pkv[sl][:, a, 0:128], lhsT=hT[:, kc, gt * 128:(gt + 1) * 128],
                                                               rhs=wq[ws][:, kc, :], start=(kc == 0), stop=(kc == KC - 1)),
                                        [bhT[gt], bwq[ws]], [bpkv[sl]], signal=(a == 1 and kc == KC - 1))
                          qsrc = pkv[sl][:, :, 0:128].rearrange("p a (h c) -> p a h c", h=2)
                          normrope(qsrc, gq_b, Cgq, Sgq, is_moba, gt0, sl, qtok[:, 2 * j:2 * j + 2, :, 0:64],
                                   [bpkv[sl]], [bqtok[2 * j], bqtok[2 * j + 1]])
                          if not is_moba:
                              DVE.op(lambda: V.tensor_copy(out=qtok[:, 2 * j:2 * j + 2, :, 64:65],
                                                           in_=cum[:, gt0:gt0 + 2, fh0:fh0 + 2].unsqueeze(3)),
                                     [bcum], [bqtok[2 * j], bqtok[2 * j + 1]])

                      def q_stage2(j):
                          ncols = 64 if is_moba else 66
                          for hd in range(2):
                              for s2 in range(2):
                                  PE.op(lambda hd=hd, s2=s2: T.transpose(
                                      out=pQT[0:ncols, hd, s2 * 128:(s2 + 1) * 128],
                                      in_=qtok[:, 2 * j + s2, hd, 0:ncols], identity=identb[:]),
                                      [bqtok[2 * j + s2], bconst], [bpQT], signal=(hd == 1 and s2 == 1))
                          if j % 2 == 0:
                              ACT.op(lambda: A.copy(out=QTall[0:ncols, :, j * 256:(j + 1) * 256], in_=pQT[0:ncols, :, :]),
                                     [bpQT], [bQT[j]])
                          else:
                              DVE.op(lambda: V.tensor_copy(out=QTall[0:ncols, :, j * 256:(j + 1) * 256], in_=pQT[0:ncols, :, :]),
                                     [bpQT], [bQT[j]])

                      for jj in range(NQB + 2):
                          if jj < NQB:
                              q_stage1(jj)
                          if jj >= 2:
                              q_stage2(jj - 2)
                      cut(52)
                      fw.barrier()
                      cut(5)

                  with contextlib.ExitStack() as sA:
                      ps = [pm("ps%d" % i, [128, 512], stack=sA) for i in range(3)]
                      po = [pm("po%d" % i, [128, 512], stack=sA) for i in range(2)]
                      pot = pm("pot", [128, 4, 65], stack=sA)
                      pg = pm("pg", [128, 2, 4 * NB], stack=sA)
                      pQ2 = pm("pQ2", [AUG, 2, 256], BF16, stack=sA)
                      bps = [Buf() for _ in range(3)]; bpo = [Buf(), Buf()]; bpot = Buf(); _bpg = Buf(); bpg = [_bpg, _bpg]; bpQ2 = Buf()

                      def gate_a(j):
                          gs_ = j % 2
                          pgv = pg[:, gs_, :].rearrange("p (s h n) -> p s h n", s=2, h=2)
                          for hd in range(2):
                              for s2 in range(2):
                                  PE.op(lambda hd=hd, s2=s2: T.matmul(
                                      pgv[:, s2, hd, :], lhsT=QTall[0:64, hd, (2 * j + s2) * 128:(2 * j + s2 + 1) * 128],
                                      rhs=kmT[hd][:], start=True, stop=True),
                                      [bQT[j], bkmT[hd]], [bpg[gs_]], signal=(hd == 1 and s2 == 1))
                          gmj = gmaskb[:, j * NB:(j + 1) * NB].unsqueeze(1).to_broadcast([128, 4, NB])
                          pvj = pvb[:, j * NB:(j + 1) * NB].unsqueeze(1).to_broadcast([128, 4, NB])
                          ncj = ncb[:, j * NB:(j + 1) * NB].unsqueeze(1).to_broadcast([128, 4, NB])
                          DVE.op(lambda: V.tensor_tensor(out=gms[gs_][:], in0=pg[:, gs_, :].rearrange("p (a n) -> p a n", n=NB),
                                                         in1=gmj, op=ALU.add), [bpg[gs_], bconst], [bgms[gs_]])
                          for a in range(4):
                              DVE.op(lambda a=a: V.max(out=mx8[gs_][:, a, :], in_=gms[gs_][:, a, :]), [bgms[gs_]], [bmx8[gs_]])
                          for a in range(4):
                              DVE.op(lambda a=a: V.tensor_scalar(out=nsf[gs_][:, a, :], in0=gms[gs_][:, a, :],
                                                                 scalar1=mx8[gs_][:, a, 2:3], scalar2=NEG,
                                                                 op0=ALU.is_lt, op1=ALU.mult),
                                     [bgms[gs_], bmx8[gs_]], [bnsf[gs_]])
                          DVE.op(lambda: V.tensor_tensor(out=nsf[gs_][:], in0=nsf[gs_][:], in1=pvj, op=ALU.mult),
                                 [bnsf[gs_], bconst], [bnsf[gs_]])
                          DVE.op(lambda: V.tensor_tensor(out=qtok[:, 2 * j:2 * j + 2, :, 64:64 + NB],
                                                         in0=nsf[gs_][:].rearrange("p (s h) n -> p s h n", s=2),
                                                         in1=ncj.rearrange("p (s h) n -> p s h n", s=2), op=ALU.add),
                                 [bnsf[gs_], bconst], [bqtok[2 * j], bqtok[2 * j + 1]])

                      def gate_b(j):
                          for hd in range(2):
                              for s2 in range(2):
                                  PE.op(lambda hd=hd, s2=s2: T.transpose(
                                      out=pQ2[0:KA, hd, s2 * 128:(s2 + 1) * 128],
                                      in_=qtok[:, 2 * j + s2, hd, 0:KA], identity=identb[:]),
                                      [bqtok[2 * j + s2], bconst], [bpQ2], signal=(hd == 1 and s2 == 1))
                          DVE.op(lambda: V.tensor_copy(out=QTall[0:KA, :, j * 256:(j + 1) * 256], in_=pQ2[0:KA, :, :]),
                                 [bpQ2], [bQT[j]])

                      if is_moba:
                          gate_a(0); gate_a(1); gate_b(0); gate_b(1)
                      cnt = 0; ocnt = 0
                      pending = []

                      def epilogue(hd, jp, osl, hg):
                          DVE.op(lambda: V.tensor_copy(out=osb[osl][:], in_=po[osl][0:65, :]), [bpo[osl]], [bosb[osl]])
                          for s2 in range(4):
                              PE.op(lambda s2=s2: T.transpose(out=pot[:, s2, :], in_=osb[osl][:, s2 * 128:(s2 + 1) * 128],
                                                              identity=identf[0:65, 0:65]),
                                    [bosb[osl], bconst], [bpot], signal=(s2 == 3))
                          DVE.op(lambda: V.reciprocal(out=rden[:].unsqueeze(2), in_=pot[:, :, 64:65]), [bpot], [brden])
                          for s2 in range(4):
                              DVE.op(lambda s2=s2: V.tensor_scalar(out=Obuf[:, 2 * jp + s2, hg * 64:(hg + 1) * 64],
                                                                   in0=pot[:, s2, 0:64], scalar1=rden[:, s2:s2 + 1],
                                                                   scalar2=None, op0=ALU.mult),
                                     [bpot, brden], [bO[2 * jp + s2]])

                      for jp in range(0, NQB, 2):
                          for hd in range(2):
                              osl = ocnt % 2; ocnt += 1
                              hg = hbase + hd
                              nk0 = 4 * jp + 4; nk1 = 4 * jp + 8
                              q0 = jp * 256

                              def score(kt, c):
                                  k3 = c % 3
                                  lhsK = KTall[:, hd, kt * 128:(kt + 1) * 128]
                                  rd = [bKT[kt // 4], bKTaug, bQT[jp], bQT[jp + 1]]
                                  qa = QTall[:, hd, q0:q0 + 256]
                                  qb = QTall[:, hd, q0 + 256:q0 + 512]
                                  if kt < nk0 - 2:
                                      PE.op(lambda: T.matmul(ps[k3][:, 0:512], lhsT=lhsK, rhs=QTall[:, hd, q0:q0 + 512],
                                                             start=True, stop=True), rd, [bps[k3]], signal=True)
                                  elif kt < nk0:
                                      d = kt - (nk0 - 2)
                                      PE.op(lambda: T.matmul(ps[k3][:, 0:256], lhsT=lhsK, rhs=qa, start=True, stop=False),
                                            rd, [bps[k3]], signal=False)
                                      PE.op(lambda: T.matmul(ps[k3][:, 0:256], lhsT=identb[:], rhs=dmaskb[:, d, :],
                                                             start=False, stop=True), [bconst], [bps[k3]], signal=False)
                                      PE.op(lambda: T.matmul(ps[k3][:, 256:512], lhsT=lhsK, rhs=qb, start=True, stop=True),
                                            rd, [bps[k3]], signal=True)
                                  elif kt < nk1 - 2:
                                      PE.op(lambda: T.matmul(ps[k3][:, 256:512], lhsT=lhsK, rhs=qb, start=True, stop=True),
                                            rd, [bps[k3]], signal=True)
                                  else:
                                      d = kt - (nk1 - 2)
                                      PE.op(lambda: T.matmul(ps[k3][:, 256:512], lhsT=lhsK, rhs=qb, start=True, stop=False),
                                            rd, [bps[k3]], signal=False)
                                      PE.op(lambda: T.matmul(ps[k3][:, 256:512], lhsT=identb[:], rhs=dmaskb[:, d, :],
                                                             start=False, stop=True), [bconst], [bps[k3]], signal=True)

                              score(0, cnt)
                              score(1, cnt + 1)
                              for kt in range(nk1):
                                  c = cnt + kt; k3 = c % 3
                                  if kt + 2 < nk1:
                                      score(kt + 2, c + 2)
                                  lo = 0 if kt < nk0 else 256
                                  if is_moba:
                                      ACT.op(lambda: A.activation(out=pTs[k3][:, lo:512], in_=ps[k3][:, lo:512],
                                                                  func=AF.Exp, scale=0.125),
                                             [bps[k3]], [bpTs[k3]])
                                  else:
                                      ACT.op(lambda: A.activation(out=pTs[k3][:, lo:512], in_=ps[k3][:, lo:512],
                                                                  func=AF.Exp, scale=0.125,
                                                                  bias=negcum[:, kt, fh0 + hd:fh0 + hd + 1]),
                                             [bps[k3], bcum], [bpTs[k3]])
                                  vl = VApf[:, (kt * 2 + hd) * 80:(kt * 2 + hd) * 80 + 128]
                                  PE.op(lambda: T.matmul(po[osl][:, lo:512], lhsT=vl, rhs=pTs[k3][:, lo:512],
                                                         start=(kt == 0), stop=(kt == nk1 - 1)),
                                        [bVA[kt], bpTs[k3]], [bpo[osl]], signal=True)
                                  if kt == 1 and pending:
                                      pending.pop()()
                                  if is_moba and kt == 2 and jp + 2 < NQB:
                                      if hd == 0:
                                          gate_a(jp + 2); gate_a(jp + 3)
                                      else:
                                          gate_b(jp + 2); gate_b(jp + 3)
                              cnt += nk1
                              if pending:
                                  pending.pop()()
                              pending.append(lambda hd=hd, jp=jp, osl=osl, hg=hg: epilogue(hd, jp, osl, hg))
                      if pending:
                          pending.pop()()
                      fw.barrier()
                      cut(6)
              fw.barrier()
              cut(7)

          with contextlib.ExitStack() as sEF:
              h2T = sb("h2T", [128, KC, NOT * 128], BF16, stack=sEF)
              bh2T = [Buf() for _ in range(NOT)]
              stat2 = sb("stat2", [128, 16], stack=sEF)
              bst2 = [Buf() for _ in range(16)]
              with contextlib.ExitStack() as sE:
                  NSL = 4
                  woutb = sb("woutb", [128, KC, D], BF16, stack=sE)
                  xo = [sb("xo%d" % i, [128, D], stack=sE) for i in range(NSL)]
                  junkO = sb("junkO", [128, 2, Mw], stack=sE); junkX = sb("junkX", [128, D], stack=sE)
                  bjO = [Buf(), Buf()]; bjX = Buf()
                  mixb = [sb("mixb%d" % i, [128, D], BF16, stack=sE) for i in range(NSL)]
                  mixT = [sb("mixT%d" % i, [128, KC, 128], BF16, stack=sE) for i in range(NSL)]
                  x1t = [sb("x1t%d" % i, [128, D], stack=sE) for i in range(NSL)]
                  xn2 = [sb("xn2%d" % i, [128, D], stack=sE) for i in range(NSL)]
                  stE = sb("stE", [128, NSL, 8], stack=sE)
                  bsE = [[Buf() for _ in range(6)] for _ in range(NSL)]
                  pmt = [pm("pmt%d" % i, [128, KC * 128], BF16, stack=sE) for i in range(2)]
                  py = [pm("py%d" % i, [128, CW], stack=sE) for i in range(2)]
                  ptr2 = pm("ptr2", [128, KC * 128], BF16, stack=sE)
                  mbm = sb("mbm", [128, 2, D], stack=sE); bmbm = Buf()
                  hb2 = [sb("hb2%d" % i, [128, D], BF16, stack=sE) for i in range(NSL)]
                  bhb2 = [Buf() for _ in range(NSL)]
                  SP.dma(mbm[:, 0, :], modscr[0:1, 2 * D:3 * D].partition_broadcast(128), reads=[bmodscr], writes=[bmbm])
                  SP.dma(mbm[:, 1, :], modscr[0:1, 3 * D:4 * D].partition_broadcast(128), reads=[bmodscr], writes=[bmbm])
                  bwout = Buf(); bxo = [Buf() for _ in range(NSL)]; bmixb = [Buf() for _ in range(NSL)]
                  bmixT = [Buf() for _ in range(NSL)]; bx1t = [Buf() for _ in range(NSL)]; bxn2 = [Buf() for _ in range(NSL)]
                  bpmt = [Buf(), Buf()]; bpy = [Buf(), Buf()]; bptr2 = Buf()
                  POOL.dma(woutb[:], w_out_v[:, :, :], writes=[bwout])
                  ycn = [0]

                  def e_s1(i):
                      sl = i % NSL; j = i // 2; s_ = i % 2
                      gt = 4 * j + 2 + s_
                      ps_ = i % 2
                      SP.dma(xo[sl][:], xs[gt * 128:(gt + 1) * 128, :], writes=[bxo[sl]])
                      for g in range(2):
                          ACT.op(lambda g=g: A.activation(out=junkO[:, g, :], in_=Obuf[:, i, g * Mw:(g + 1) * Mw],
                                                          func=AF.Square, accum_out=stE[:, sl, g:g + 1]),
                                 [bO[i]], [bjO[g], bsE[sl][g]])
                      ACT.op(lambda: A.activation(out=stE[:, sl, 2:4], in_=stE[:, sl, 0:2], func=AF.Ln, scale=1.0 / Mw,
                                                  bias=epsb[:, 0:1]), [bsE[sl][0], bsE[sl][1], bepsb], [bsE[sl][2]])
                      ACT.op(lambda: A.activation(out=stE[:, sl, 2:4], in_=stE[:, sl, 2:4], func=AF.Exp, scale=-0.5),
                             [bsE[sl][2]], [bsE[sl][2]])
                      for g in range(2):
                          DVE.op(lambda g=g: V.scalar_tensor_tensor(
                              out=mixb[sl][:, g * Mw:(g + 1) * Mw], in0=Obuf[:, i, g * Mw:(g + 1) * Mw],
                              scalar=stE[:, sl, 2 + g:3 + g], in1=goutbc[:, g * Mw:(g + 1) * Mw],
                              op0=ALU.mult, op1=ALU.mult), [bO[i], bsE[sl][2], bconst], [bmixb[sl]])

                  def e_s1b(i):
                      sl = i % NSL
                      ps_ = i % 2
                      for kc in range(KC):
                          PE.op(lambda kc=kc: T.transpose(out=pmt[ps_][:, kc * 128:(kc + 1) * 128],
                                                          in_=mixb[sl][:, kc * 128:(kc + 1) * 128], identity=identb[:]),
                                [bmixb[sl], bconst], [bpmt[ps_]], signal=(kc == KC - 1))
                      ACT.op(lambda: A.copy(out=mixT[sl][:].rearrange("p k c -> p (k c)"), in_=pmt[ps_][:]),
                             [bpmt[ps_]], [bmixT[sl]])

                  def e_s2(i):
                      sl = i % NSL
                      for c2 in range(NC2):
                          ys = ycn[0] % 2; ycn[0] += 1
                          for kc in range(KC):
                              PE.op(lambda kc=kc: T.matmul(py[ys][:], lhsT=mixT[sl][:, kc, :],
                                                           rhs=woutb[:, kc, c2 * CW:(c2 + 1) * CW],
                                                           start=(kc == 0), stop=(kc == KC - 1)),
                                    [bmixT[sl], bwout], [bpy[ys]], signal=(kc == KC - 1))
                          DVE.op(lambda: V.tensor_tensor(out=x1t[sl][:, c2 * CW:(c2 + 1) * CW], in0=py[ys][:],
                                                         in1=gbc[:, 0, c2 * CW:(c2 + 1) * CW], op=ALU.mult),
                                 [bpy[ys], bgbc], [bx1t[sl]])
                          POOL.op(lambda: G.tensor_tensor(out=x1t[sl][:, c2 * CW:(c2 + 1) * CW],
                                                          in0=x1t[sl][:, c2 * CW:(c2 + 1) * CW],
                                                          in1=xo[sl][:, c2 * CW:(c2 + 1) * CW], op=ALU.add),
                                  [bx1t[sl], bxo[sl]], [bx1t[sl]])
                      SP.dma(out[i * 128:(i + 1) * 128, :], x1t[sl][:], reads=[bx1t[sl]], writes=[bout[i]])
                      ACT.op(lambda: A.activation(out=junkX[:], in_=x1t[sl][:], func=AF.Square,
                                                  accum_out=stE[:, sl, 4:5]), [bx1t[sl]], [bjX, bsE[sl][3]])
                      rstd_ops(stE[:, sl, 6:7], stE[:, sl, 4:5], D, bsE[sl][3], bsE[sl][5], stE[:, sl, 5:6], bsE[sl][4])
                      DVE.op(lambda: V.scalar_tensor_tensor(out=xn2[sl][:], in0=x1t[sl][:], scalar=stE[:, sl, 6:7],
                                                            in1=mbm[:, 1, :], op0=ALU.mult, op1=ALU.mult),
                             [bx1t[sl], bsE[sl][5], bmbm], [bxn2[sl]])
                      DVE.op(lambda: V.tensor_tensor(out=hb2[sl][:], in0=xn2[sl][:], in1=mbm[:, 0, :], op=ALU.add),
                             [bxn2[sl], bmbm], [bhb2[sl]])

                  def e_s3(i):
                      sl = i % NSL
                      for kc in range(KC):
                          PE.op(lambda kc=kc: T.transpose(out=ptr2[:, kc * 128:(kc + 1) * 128],
                                                          in_=hb2[sl][:, kc * 128:(kc + 1) * 128], identity=identb[:]),
                                [bhb2[sl], bconst], [bptr2], signal=(kc == KC - 1))
                      ACT.op(lambda: A.copy(out=h2T[:, :, i * 128:(i + 1) * 128],
                                            in_=ptr2[:].rearrange("p (k c) -> p k c", k=KC)),
                             [bptr2], [bh2T[i]])

                  for k in range(NOT + 3):
                      if k < NOT:
                          e_s1(k)
                      if 0 <= k - 1 < NOT:
                          e_s1b(k - 1)
                      if 0 <= k - 2 < NOT:
                          e_s2(k - 2)
                      if 0 <= k - 3 < NOT:
                          e_s3(k - 3)
                  fw.barrier()
                  cut(8)

              with contextlib.ExitStack() as sF:
                  HW_ = HT * 128
                  NG = HW_ // 512
                  NX = min(4, HT)
                  uT = sb("uT", [128, FC, HW_], BF16, stack=sF)
                  w1s = [sb("w1s%d" % i, [128, KC, 128], BF16, stack=sF) for i in range(4)]
                  w2s = [sb("w2s%d" % i, [128, 2, CW], BF16, stack=sF) for i in range(4)]
                  w2scr_v = w2scr.rearrange("(fc p) c -> p fc c", p=128)
                  relu = [sb("relu%d" % i, [128, 512], stack=sF) for i in range(2)]
                  x1r = [sb("x1r%d" % i, [128, CW], stack=sF) for i in range(NX)]
                  zo = [sb("zo%d" % i, [128, CW], stack=sF) for i in range(2)]
                  buT = [[Buf() for _ in range(NG)] for _ in range(FC)]
                  bw1 = [Buf() for _ in range(4)]; bw2 = [Buf() for _ in range(4)]
                  brelu = [Buf(), Buf()]; bx1r = [Buf() for _ in range(NX)]; bzo = [Buf(), Buf()]
                  w1c = 0; w2c = 0; rc = 0; zc = 0
                  for hh in range(2):
                      with contextlib.ExitStack() as s1:
                          pu = [pm("pu%d" % i, [128, 512], stack=s1) for i in range(4)]
                          bpu = [Buf() for _ in range(4)]
                          puc = 0
                          for fc in range(FC):
                              wsl = w1c % 4; w1c += 1
                              SP.dma(w1s[wsl][:].rearrange("p k c -> p (k c)"), w1scr[fc, :, :], reads=[bw1scr], writes=[bw1[wsl]])
                              for g in range(NG):
                                  pk = puc % 4; puc += 1
                                  tok0 = hh * HW_ + g * 512
                                  for kc in range(KC):
                                      PE.op(lambda kc=kc: T.matmul(pu[pk][:], lhsT=w1s[wsl][:, kc, :],
                                                                   rhs=h2T[:, kc, tok0:tok0 + 512],
                                                                   start=(kc == 0), stop=(kc == KC - 1)),
                                            [bw1[wsl]] + bh2T[tok0 // 128: tok0 // 128 + 4], [bpu[pk]],
                                            signal=(kc == KC - 1))
                                  rs = rc % 2; rc += 1
                                  ACT.op(lambda: A.activation(out=relu[rs][:], in_=pu[pk][:], func=AF.Relu),
                                         [bpu[pk]], [brelu[rs]])
                                  DVE.op(lambda: V.tensor_tensor(out=uT[:, fc, g * 512:(g + 1) * 512], in0=relu[rs][:],
                                                                 in1=relu[rs][:], op=ALU.mult),
                                         [brelu[rs]], [buT[fc][g]])
                          fw.barrier()
                          cut(9)
                      with contextlib.ExitStack() as s2:
                          acc = [pm("acc%d" % i, [128, CW], stack=s2) for i in range(HT)]
                          bacc = [Buf() for _ in range(HT)]
                          for c2 in range(NC2):
                              for tt in range(NX):
                                  i = hh * HT + tt
                                  ACT.dma(x1r[tt % NX][:], out[i * 128:(i + 1) * 128, c2 * CW:(c2 + 1) * CW],
                                          reads=[bout[i]], writes=[bx1r[tt % NX]])
                              for fg in range(FC // 2):
                                  wsl = w2c % 4; w2c += 1
                                  SP.dma(w2s[wsl][:], w2scr_v[:, fg * 2:(fg + 1) * 2, c2 * CW:(c2 + 1) * CW], reads=[bw2scr], writes=[bw2[wsl]])
                                  for fl in range(2):
                                      fc = fg * 2 + fl
                                      for tt in range(HT):
                                          PE.op(lambda fl=fl, tt=tt: T.matmul(acc[tt][:], lhsT=uT[:, fc, tt * 128:(tt + 1) * 128],
                                                                              rhs=w2s[wsl][:, fl, :],
                                                                              start=(fc == 0), stop=(fc == FC - 1)),
                                                [buT[fc][tt // 4], bw2[wsl]], [bacc[tt]],
                                                signal=(fc == FC - 1 or (fl == 1 and tt == HT - 1)))
                              for tt in range(HT):
                                  i = hh * HT + tt
                                  zs = zc % 2; zc += 1
                                  xs_ = tt % NX
                                  DVE.op(lambda: V.tensor_tensor(out=zo[zs][:], in0=acc[tt][:],
                                                                 in1=gbc[:, 1, c2 * CW:(c2 + 1) * CW], op=ALU.mult),
                                         [bacc[tt], bgbc], [bzo[zs]])
                                  DVE.op(lambda: V.tensor_tensor(out=zo[zs][:], in0=zo[zs][:], in1=x1r[xs_][:], op=ALU.add),
                                         [bzo[zs], bx1r[xs_]], [bzo[zs]])
                                  ACT.dma(out[i * 128:(i + 1) * 128, c2 * CW:(c2 + 1) * CW], zo[zs][:],
                                          reads=[bzo[zs]], writes=[bout[i]])
                                  if tt + NX < HT:
                                      i2 = hh * HT + tt + NX
                                      ACT.dma(x1r[xs_][:], out[i2 * 128:(i2 + 1) * 128, c2 * CW:(c2 + 1) * CW],
                                              reads=[bout[i2]], writes=[bx1r[xs_]])
                          fw.barrier()
                          cut(10)
                  fw.barrier()
      except _Stop:
        pass
      fw.stopped = False
      SP = fw.sp
      SP.drain_dmas()
      fw.act.drain_dmas()
      fw.pool.drain_dmas()
      fw.barrier()
    return nc


def _host_consts(cfg, h):
    S = cfg.S
    NT = S // 128; NB = S // 256; NQB = NB // 2
    NAUG = max(NB, 2)
    ident = np.eye(128, dtype=np.float32)
    ss, tt = np.meshgrid(np.arange(128), np.arange(128), indexing="ij")
    utri = (ss <= tt).astype(np.float32)
    kk = np.arange(128)[:, None, None]; dd = np.arange(2)[None, :, None]; qq = np.arange(256)[None, None, :]
    dmask = np.where(dd * 128 + kk > qq, NEG, 0.0).astype(np.float32).reshape(128, 512)
    pos = np.arange(S, dtype=np.float64) - (256.0 if h == 0 else 0.0)
    pos = np.maximum(pos, 0.0)
    inv_freq = ROPE_THETA ** (-np.arange(0, 16, 2, dtype=np.float64) / 16.0)
    ang = pos[:, None] * inv_freq[None, :]
    cs = np.concatenate([np.cos(ang), np.cos(ang), -np.sin(ang), np.sin(ang)], axis=1).astype(np.float32)
    cs = np.ascontiguousarray(cs.reshape(NT, 128, 32).transpose(1, 0, 2).reshape(128, NT * 32))
    onehot = np.zeros((NAUG, S), np.float32)
    for m in range(NB):
        onehot[m, m * 256:(m + 1) * 256] = 1.0
    foxaug = np.zeros((2, S), np.float32)
    foxaug[0, :] = 8.0
    if h == 0:
        foxaug[1, 0:256] = NEG
    gmask = np.full((NQB, NB), -1e30, np.float32)
    pastvalid = np.zeros((NQB, NB), np.float32)
    negconst = np.zeros((NQB, NB), np.float32)
    for j in range(NQB):
        P = 2 * j + 1
        for n in range(NB):
            if (1 - h) <= n < P:
                gmask[j, n] = 0.0
                pastvalid[j, n] = 1.0
        if h == 0:
            negconst[j, 0] = NEG
    return dict(ident=ident, utri=utri, dmask=dmask, cs=cs, onehot=onehot, foxaug=foxaug,
                gmask=gmask.reshape(1, -1), pastvalid=pastvalid.reshape(1, -1), negconst=negconst.reshape(1, -1))


def make_in_maps(cfg, x, c, w_ada, b_ada, w_in, b_forget, g_qn_moba, g_kn_moba, g_qn_fox, g_kn_fox,
                 g_out_moba, g_out_fox, w_out, w_ff1, w_ff2):
    B, S, D = x.shape
    KC = D // 128
    f = lambda a: np.ascontiguousarray(np.asarray(a, dtype=np.float32))
    shared = dict(w_ada=f(w_ada[0]), b_ada=f(b_ada[0]).reshape(1, -1), w_in=f(w_in[0]), w_out=f(w_out[0]),
                  w_ff1=f(w_ff1[0]), w_ff2=f(w_ff2[0]), b_forget=f(b_forget[0]).reshape(1, -1),
                  g_qn_moba=f(g_qn_moba[0]).reshape(1, -1), g_kn_moba=f(g_kn_moba[0]).reshape(1, -1),
                  g_qn_fox=f(g_qn_fox[0]).reshape(1, -1), g_kn_fox=f(g_kn_fox[0]).reshape(1, -1),
                  g_out=f(np.concatenate([np.asarray(g_out_moba[0]), np.asarray(g_out_fox[0])])).reshape(1, -1))
    consts = [_host_consts(cfg, 0), _host_consts(cfg, 1)]
    in_maps = []
    x = np.asarray(x, dtype=np.float32); c = np.asarray(c, dtype=np.float32)
    for b in range(B):
        for h in range(2):
            if h == 0:
                xs = np.concatenate([np.zeros((256, D), np.float32), x[b, :S - 256]], axis=0)
            else:
                xs = x[b]
            m = dict(shared)
            m.update(consts[h])
            m["xs"] = np.ascontiguousarray(xs)
            m["ccol"] = np.ascontiguousarray(c[b].reshape(KC, 128).T)
            in_maps.append(m)
    return in_maps


def gather(cfg, results, B):
    S, D = cfg.S, cfg.D
    NQB = S // 512
    y = np.zeros((B, S, D), np.float32)
    k = 0
    for b in range(B):
        for h in range(2):
            o = np.asarray(results[k]["out"]).reshape(NQB, 256, D)
            k += 1
            for j in range(NQB):
                blk = 2 * j + h
                y[b, blk * 256:(blk + 1) * 256, :] = o[j]
    return y


_NC_CACHE = {}


def kernel(x, c, w_ada, b_ada, w_in, b_forget, g_qn_moba, g_kn_moba, g_qn_fox, g_kn_fox,
           g_out_moba, g_out_fox, w_out, w_ff1, w_ff2):
    x = np.asarray(x)
    B, S, D = x.shape
    DFF = np.asarray(w_ff1).shape[-1]
    cfg = Cfg(D, S, DFF)
    key = (D, S, DFF)
    if key not in _NC_CACHE:
        _NC_CACHE[key] = build(cfg)
    nc = _NC_CACHE[key]
    in_maps = make_in_maps(cfg, x, c, w_ada, b_ada, w_in, b_forget, g_qn_moba, g_kn_moba, g_qn_fox, g_kn_fox,
                           g_out_moba, g_out_fox, w_out, w_ff1, w_ff2)
    res = run_bass_kernel_spmd(nc, in_maps, core_ids=list(range(2 * B)))
    return gather(cfg, res.results, B)
```
